# Optimizing a Trainium2 kernel written in Bass

```python
import math
import jax, jax.numpy as jnp
from jax import lax
import numpy as np

D_MODEL = 2048
BATCH = 32
SEQ = 256
DEPTH = 2
DEC_BATCH = 2
DEC_SEQ = 2048
PAST_LEN = 256

GRID_W = 64
N_ATT_HEADS = D_MODEL // 256
D_QK = 64
D_V = 2 * D_QK
W_ATT = N_ATT_HEADS * D_V
ROPE_THETA = 10000.0
ROPE_FREQS = D_QK // 4
Q_BLOCK = 128
W_LRU = D_MODEL // 2
N_LRU_BLOCKS = 16
LRU_BS = W_LRU // N_LRU_BLOCKS
LRU_C = 8.0
LRU_CONV = 4
W_HY = D_MODEL // 2
HY_ORDER = 2
HY_CONV = 3
HY_BANDS = 8
HY_EMB = 1 + 2 * HY_BANDS
HY_HIDDEN = 64
HY_MIN_DECAY = 3.07
HY_MAX_DECAY = 15.35
N_BRANCH = 3
W_BR = W_ATT
D_FF = ((8 * D_MODEL // 3 + 255) // 256) * 256
W_IN = 3 * W_ATT + 2 * W_LRU + (HY_ORDER + 1) * W_HY
NORM_EPS = 1e-6
SUBLN_EPS = 1e-5

kernel_name = 'hybrid_diffattn_rglru_hyena_dit_step'


def rmsnorm(x, g, eps=NORM_EPS):
    xf = x.astype(jnp.float32)
    y = xf * lax.rsqrt(jnp.mean(xf * xf, axis=-1, keepdims=True) + eps)
    return (y * g.astype(jnp.float32)).astype(x.dtype)


def dwconv(x, w, b, pad_left):
    K = w.shape[0]
    L = x.shape[1]
    xp = jnp.pad(x, ((0, 0), (pad_left, K - 1 - pad_left), (0, 0)))
    y = b + w[0] * xp[:, 0:L]
    for k in range(1, K):
        y = y + w[k] * xp[:, k:k + L]
    return y


def axial_rope(L):
    n_rows = L // GRID_W
    row = jnp.repeat(jnp.arange(n_rows), GRID_W)
    col = jnp.tile(jnp.arange(GRID_W), n_rows)
    inv_freq = ROPE_THETA ** (-jnp.arange(ROPE_FREQS, dtype=jnp.float32) / ROPE_FREQS)
    pos = jnp.stack([row, col], axis=-1).astype(jnp.float32)
    ang = pos[:, :, None] * inv_freq
    return jnp.cos(ang), jnp.sin(ang)


def apply_rope(x, cos, sin):
    B, L, H, _ = x.shape
    xr = x.astype(jnp.float32).reshape(B, L, H, 2, 2, ROPE_FREQS)
    x1, x2 = xr[..., 0, :], xr[..., 1, :]
    c = cos[None, :, None]
    s = sin[None, :, None]
    out = jnp.stack([x1 * c - x2 * s, x2 * c + x1 * s], axis=-2)
    return out.reshape(B, L, H, D_QK).astype(x.dtype)


def diff_attention(q1, q2, k1, k2, v, lam):
    B, Lq, H, _ = q1.shape
    nb = Lq // Q_BLOCK
    scale = D_QK ** -0.5

    def blocks(q):
        return q.reshape(B, nb, Q_BLOCK, H, D_QK).transpose(1, 0, 2, 3, 4)

    def one_block(qb):
        qb1, qb2 = qb
        s1 = jnp.einsum('bqhd,bkhd->bhqk', qb1, k1).astype(jnp.float32) * scale
        s2 = jnp.einsum('bqhd,bkhd->bhqk', qb2, k2).astype(jnp.float32) * scale
        w = jax.nn.softmax(s1, axis=-1) - lam * jax.nn.softmax(s2, axis=-1)
        return jnp.einsum('bhqk,bkhe->bqhe', w.astype(v.dtype), v)

    out = lax.map(one_block, (blocks(q1), blocks(q2)))
    return out.transpose(1, 0, 2, 3, 4).reshape(B, Lq, H, D_V)


def _lin_combine(e1, e2):
    a1, b1 = e1
    a2, b2 = e2
    return a1 * a2, a2 * b1 + b2


def rglru_scan(x, wa, ba, wx, bx, lam, h0, reverse):
    B, L, W = x.shape
    f32 = jnp.float32
    xf = x.astype(f32)
    xb = xf.reshape(B, L, N_LRU_BLOCKS, LRU_BS)
    r = jax.nn.sigmoid(jnp.einsum('blnd,nde->blne', xb, wa.astype(f32)).reshape(B, L, W) + ba.astype(f32))
    i = jax.nn.sigmoid(jnp.einsum('blnd,nde->blne', xb, wx.astype(f32)).reshape(B, L, W) + bx.astype(f32))
    log_a = -LRU_C * r * jax.nn.softplus(-lam.astype(f32))
    a = jnp.exp(log_a)
    b = jnp.sqrt(-jnp.expm1(2.0 * log_a)) * (i * xf)
    if reverse:
        a, b = a[:, ::-1], b[:, ::-1]
    a_cum, b_cum = lax.associative_scan(_lin_combine, (a, b), axis=1)
    h = a_cum * h0.astype(f32)[:, None] + b_cum
    h_last = h[:, -1]
    if reverse:
        h = h[:, ::-1]
    return h, h_last


def hyena_filters(L, lp):
    f32 = jnp.float32
    t = jnp.linspace(0.0, 1.0, L, dtype=f32)[:, None]
    w = 2.0 * math.pi * jnp.arange(L, dtype=f32)[:, None] / L
    f = jnp.linspace(1e-4, HY_BANDS - 1, HY_BANDS, dtype=f32)[None]
    z = jnp.concatenate([t, jnp.cos(f * w), -jnp.sin(f * w)], axis=-1)
    freq = lp['hy_freq'].astype(f32)
    hdn = jnp.sin(freq * (z @ lp['hy_w1'].astype(f32) + lp['hy_b1'].astype(f32)))
    hdn = jnp.sin(freq * (hdn @ lp['hy_w2'].astype(f32) + lp['hy_b2'].astype(f32)))
    filt = (hdn @ lp['hy_w3'].astype(f32) + lp['hy_b3'].astype(f32)).reshape(L, HY_ORDER, 2, W_HY).transpose(1, 2, 0, 3)
    window = jnp.exp(-t[None, None] * jnp.abs(lp['hy_decay'].astype(f32))[:, :, None, :])
    return filt * window


def fft_bidir_conv(u, h_fwd, h_bwd):
    B, L, W = u.shape
    f = jnp.concatenate([h_fwd, jnp.zeros((1, W), jnp.float32), h_bwd[:0:-1]], axis=0)
    U = jnp.fft.rfft(u, n=2 * L, axis=1)
    F = jnp.fft.rfft(f, n=2 * L, axis=0)
    return jnp.fft.irfft(U * F[None], n=2 * L, axis=1)[:, :L]


def trunk_layer(x, cvec, lp, lam_init, rope, ctx_kv, lru_h0):
    f32 = jnp.float32
    B, L, _ = x.shape
    mod = (jax.nn.silu(cvec) @ lp['w_mod'] + lp['b_mod'])[:, None, :]
    sh1, sc1, g1, sh2, sc2, g2 = jnp.split(mod, 6, axis=-1)
    h = rmsnorm(x, lp['norm_mix']) * (1.0 + sc1) + sh1
    proj = h @ lp['w_in']
    q, k, v, xl, gl, hy = jnp.split(proj, (W_ATT, 2 * W_ATT, 3 * W_ATT, 3 * W_ATT + W_LRU, 3 * W_ATT + 2 * W_LRU), axis=-1)

    q = q.reshape(B, L, N_ATT_HEADS, 2, D_QK)
    k = k.reshape(B, L, N_ATT_HEADS, 2, D_QK)
    v = v.reshape(B, L, N_ATT_HEADS, D_V)
    q1, q2, k1, k2 = q[..., 0, :], q[..., 1, :], k[..., 0, :], k[..., 1, :]
    k_self = k.reshape(B, L, N_ATT_HEADS, 2 * D_QK)
    if rope is not None:
        cos, sin = rope
        q1 = apply_rope(q1, cos, sin)
        q2 = apply_rope(q2, cos, sin)
        k1 = apply_rope(k1, cos, sin)
        k2 = apply_rope(k2, cos, sin)
    if ctx_kv is not None:
        ck, cv = ctx_kv
        k1 = jnp.concatenate([k1, ck[..., :D_QK]], axis=1)
        k2 = jnp.concatenate([k2, ck[..., D_QK:]], axis=1)
        v_all = jnp.concatenate([v, cv], axis=1)
    else:
        v_all = v
    lam_p = lp['att_lambda'].astype(f32)
    lam = jnp.exp(jnp.sum(lam_p[0] * lam_p[1])) - jnp.exp(jnp.sum(lam_p[2] * lam_p[3])) + lam_init
    att = diff_attention(q1, q2, k1, k2, v_all, lam)
    att = (rmsnorm(att, lp['att_subln'], SUBLN_EPS) * (1.0 - lam_init)).reshape(B, L, W_ATT)

    xl = dwconv(xl, lp['lru_conv_w'], lp['lru_conv_b'], LRU_CONV // 2)
    h_f, s_f = rglru_scan(xl, lp['lru_wa'][0], lp['lru_ba'][0], lp['lru_wx'][0], lp['lru_bx'][0], lp['lru_lambda'][0], lru_h0[:, 0], False)
    h_b, s_b = rglru_scan(xl, lp['lru_wa'][1], lp['lru_ba'][1], lp['lru_wx'][1], lp['lru_bx'][1], lp['lru_lambda'][1], lru_h0[:, 1], True)
    lru = ((h_f + h_b) * jax.nn.gelu(gl.astype(f32))).astype(x.dtype)

    hy = dwconv(hy, lp['hy_conv_w'], lp['hy_conv_b'], HY_CONV // 2)
    filt = hyena_filters(L, lp)
    z = hy[..., :W_HY].astype(f32)
    for o in range(HY_ORDER):
        gate = hy[..., (o + 1) * W_HY:(o + 2) * W_HY].astype(f32)
        z = gate * (fft_bidir_conv(z, filt[o, 0], filt[o, 1]) + lp['hy_d'][o].astype(f32) * z)
    hyo = z.astype(x.dtype)

    branches = jnp.stack([att.astype(x.dtype), lru, hyo], axis=2)
    br = jnp.einsum('blnw,nwd->blnd', branches, lp['w_br'])
    gates = jax.nn.sigmoid(h @ lp['w_gate']).reshape(B, L, N_BRANCH, D_MODEL)
    mixed = jnp.sum(gates * br, axis=2) @ lp['w_out']
    x = x + g1 * mixed

    h2 = rmsnorm(x, lp['norm_ffn']) * (1.0 + sc2) + sh2
    ffn = (jax.nn.silu(h2 @ lp['w_ff_gate']) * (h2 @ lp['w_ff_up'])) @ lp['w_ff_down']
    x = x + g2 * ffn
    return x, k_self, v, jnp.stack([s_f, s_b], axis=1).astype(x.dtype)


def setup_inputs(seed: int = 0) -> dict:
    key = jax.random.key(seed)
    ks = iter(jax.random.split(key, 48))
    f32 = jnp.float32
    D = D_MODEL

    def nrm(shape, scale):
        return jax.random.normal(next(ks), shape, f32) * scale

    def gain(shape):
        return 1.0 + nrm(shape, 0.02)

    a_lru = jax.random.uniform(next(ks), (DEPTH, 2, W_LRU), f32, 0.9, 0.999)
    hy_decay = jax.random.uniform(next(ks), (DEPTH, HY_ORDER, 2, W_HY), f32, HY_MIN_DECAY, HY_MAX_DECAY)
    return {
        'x_prompt': nrm((BATCH, SEQ, D), 1.0),
        'x_sample': nrm((DEC_BATCH, DEC_SEQ, D), 1.0),
        'c': nrm((DEC_BATCH, D), 1.0),
        'cache_k': nrm((DEC_BATCH, DEPTH, PAST_LEN, N_ATT_HEADS, 2 * D_QK), 1.0),
        'cache_v': nrm((DEC_BATCH, DEPTH, PAST_LEN, N_ATT_HEADS, D_V), 1.0),
        'state_lru': nrm((DEC_BATCH, DEPTH, 2, W_LRU), 0.5),
        'c_ctx': nrm((D,), 1.0),
        'w_mod': nrm((DEPTH, D, 6 * D), 0.5 * D ** -0.5),
        'b_mod': nrm((DEPTH, 6 * D), 0.02),
        'norm_mix': gain((DEPTH, D)),
        'norm_ffn': gain((DEPTH, D)),
        'w_in': nrm((DEPTH, D, W_IN), D ** -0.5),
        'w_gate': nrm((DEPTH, D, N_BRANCH * D), D ** -0.5),
        'att_lambda': nrm((DEPTH, 4, D_QK), 0.1),
        'att_subln': gain((DEPTH, D_V)),
        'lru_conv_w': nrm((DEPTH, LRU_CONV, W_LRU), LRU_CONV ** -0.5),
        'lru_conv_b': nrm((DEPTH, W_LRU), 0.02),
        'lru_wa': nrm((DEPTH, 2, N_LRU_BLOCKS, LRU_BS, LRU_BS), LRU_BS ** -0.5),
        'lru_ba': nrm((DEPTH, 2, W_LRU), 0.02),
        'lru_wx': nrm((DEPTH, 2, N_LRU_BLOCKS, LRU_BS, LRU_BS), LRU_BS ** -0.5),
        'lru_bx': nrm((DEPTH, 2, W_LRU), 0.02),
        'lru_lambda': jnp.log(a_lru) - jnp.log1p(-a_lru),
        'hy_conv_w': nrm((DEPTH, HY_CONV, (HY_ORDER + 1) * W_HY), HY_CONV ** -0.5),
        'hy_conv_b': nrm((DEPTH, (HY_ORDER + 1) * W_HY), 0.02),
        'hy_w1': nrm((DEPTH, HY_EMB, HY_HIDDEN), HY_EMB ** -0.5),
        'hy_b1': nrm((DEPTH, HY_HIDDEN), 0.02),
        'hy_w2': nrm((DEPTH, HY_HIDDEN, HY_HIDDEN), HY_HIDDEN ** -0.5),
        'hy_b2': nrm((DEPTH, HY_HIDDEN), 0.02),
        'hy_w3': nrm((DEPTH, HY_HIDDEN, HY_ORDER * 2 * W_HY), 0.01),
        'hy_b3': nrm((DEPTH, HY_ORDER * 2 * W_HY), 0.002),
        'hy_freq': gain((DEPTH, HY_HIDDEN)),
        'hy_decay': hy_decay,
        'hy_d': nrm((DEPTH, HY_ORDER, W_HY), 1.0),
        'w_br': nrm((DEPTH, N_BRANCH, W_BR, D), W_BR ** -0.5),
        'w_out': nrm((DEPTH, D, D), D ** -0.5),
        'w_ff_gate': nrm((DEPTH, D, D_FF), D ** -0.5),
        'w_ff_up': nrm((DEPTH, D, D_FF), D ** -0.5),
        'w_ff_down': nrm((DEPTH, D_FF, D), D_FF ** -0.5),
        'final_norm': gain((D,)),
    }


def reference(x_prompt, x_sample, c, cache_k, cache_v, state_lru, c_ctx, w_mod, b_mod, norm_mix, norm_ffn, w_in, w_gate, att_lambda, att_subln, lru_conv_w, lru_conv_b, lru_wa, lru_ba, lru_wx, lru_bx, lru_lambda, hy_conv_w, hy_conv_b, hy_w1, hy_b1, hy_w2, hy_b2, hy_w3, hy_b3, hy_freq, hy_decay, hy_d, w_br, w_out, w_ff_gate, w_ff_up, w_ff_down, final_norm):
    weights = dict(w_mod=w_mod, b_mod=b_mod, norm_mix=norm_mix, norm_ffn=norm_ffn, w_in=w_in, w_gate=w_gate,
                   att_lambda=att_lambda, att_subln=att_subln, lru_conv_w=lru_conv_w, lru_conv_b=lru_conv_b,
                   lru_wa=lru_wa, lru_ba=lru_ba, lru_wx=lru_wx, lru_bx=lru_bx, lru_lambda=lru_lambda,
                   hy_conv_w=hy_conv_w, hy_conv_b=hy_conv_b, hy_w1=hy_w1, hy_b1=hy_b1, hy_w2=hy_w2, hy_b2=hy_b2,
                   hy_w3=hy_w3, hy_b3=hy_b3, hy_freq=hy_freq, hy_decay=hy_decay, hy_d=hy_d, w_br=w_br, w_out=w_out,
                   w_ff_gate=w_ff_gate, w_ff_up=w_ff_up, w_ff_down=w_ff_down)

    xp = x_prompt
    h0_ctx = jnp.zeros((x_prompt.shape[0], 2, W_LRU), x_prompt.dtype)
    ks, vs, ss = [], [], []
    for l in range(DEPTH):
        lp = {name: arr[l] for name, arr in weights.items()}
        lam_init = 0.8 - 0.6 * math.exp(-0.3 * l)
        xp, k_l, v_l, s_l = trunk_layer(xp, c_ctx[None], lp, lam_init, None, None, h0_ctx)
        ks.append(k_l)
        vs.append(v_l)
        ss.append(s_l)
    y_prompt = rmsnorm(xp, final_norm)
    new_cache_k = jnp.stack(ks, axis=1)
    new_cache_v = jnp.stack(vs, axis=1)
    new_state_lru = jnp.stack(ss, axis=1)

    xs = x_sample
    rope = axial_rope(x_sample.shape[1])
    for l in range(DEPTH):
        lp = {name: arr[l] for name, arr in weights.items()}
        lam_init = 0.8 - 0.6 * math.exp(-0.3 * l)
        xs, _, _, _ = trunk_layer(xs, c, lp, lam_init, rope, (cache_k[:, l], cache_v[:, l]), state_lru[:, l])
    y_sample = rmsnorm(xs, final_norm)
    return (y_prompt, y_sample, new_cache_k, new_cache_v, new_state_lru)
```

```python
import math
from contextlib import ExitStack
import numpy as np
import ml_dtypes
import concourse.bass as bass
import concourse.mybir as mybir
from concourse.bass_utils import run_bass_kernel_spmd

F32 = mybir.dt.float32
BF16 = mybir.dt.bfloat16
AF = mybir.ActivationFunctionType
ALU = mybir.AluOpType

D = 2048
KC = 16
NH = 8
DFF = 5632
FC = 44
LAYERS = 2
NPS = 4
LP = 256
LS = 2048
PAST = 256
NCORES = 8
LAM_INIT = [0.8 - 0.6 * math.exp(-0.3 * l) for l in range(LAYERS)]


class Buf:
    __slots__ = ("name", "last_cw", "wr_dma", "rd_eng", "rd_dma", "sem", "dma_count", "is_holder")

    def __init__(self, name):
        self.name = name
        self.last_cw = None
        self.wr_dma = set()
        self.rd_eng = {}
        self.rd_dma = set()
        self.sem = None
        self.dma_count = 0
        self.is_holder = False


class Op:
    __slots__ = ("eng", "fn", "is_dma", "cc", "holder", "marked", "val", "w_eng", "w_buf", "_idx")


class V:
    __slots__ = ("ap", "bufs")

    def __init__(self, ap, bufs):
        self.ap = ap
        self.bufs = bufs

    def __getitem__(self, idx):
        return V(self.ap[idx], self.bufs)

    def rr(self, pat, **kw):
        return V(self.ap.rearrange(pat, **kw), self.bufs)

    def bc(self, shape):
        return V(self.ap.to_broadcast(shape), self.bufs)


class Prog:
    ENGS = ("pe", "act", "dve", "pool", "sp")

    def __init__(self, nc):
        self.nc = nc
        self.ops = {e: [] for e in self.ENGS}
        self.holders = []

    def op(self, eng, fn, reads=(), writes=(), dma=False, holder=None, part=False, cc=False):
        o = Op()
        o.eng = eng
        o.fn = fn
        o.is_dma = dma
        o.cc = cc
        o.holder = None
        o.marked = False
        o.val = 0
        o._idx = len(self.ops[eng])
        w_eng = {}
        w_buf = {}

        def add_dep(d):
            if d is None:
                return
            if d.eng == eng and eng == "pe":
                return
            cur = w_eng.get(d.eng)
            if cur is None or cur._idx < d._idx:
                w_eng[d.eng] = d

        for b in reads:
            add_dep(b.last_cw)
            for h in b.wr_dma:
                w_buf[h] = h.dma_count
        for b in writes:
            add_dep(b.last_cw)
            if not dma:
                for h in b.wr_dma:
                    w_buf[h] = h.dma_count
            for d in b.rd_eng.values():
                add_dep(d)
            for h in b.rd_dma:
                w_buf[h] = h.dma_count
        for d in w_eng.values():
            d.marked = True
        o.w_eng = w_eng
        o.w_buf = w_buf
        for b in reads:
            if dma:
                b.rd_dma.add(holder)
            else:
                b.rd_eng[eng] = o
        for b in writes:
            b.rd_eng = {}
            b.rd_dma = set()
            if dma:
                if part:
                    b.wr_dma.add(holder)
                else:
                    b.wr_dma = {holder}
            else:
                b.last_cw = o
                b.wr_dma = set()
        if dma:
            if not holder.is_holder:
                holder.is_holder = True
                self.holders.append(holder)
            holder.dma_count += (1 if cc else 16)
            o.holder = holder
        self.ops[eng].append(o)
        return o

    def emit(self, es):
        nc = self.nc
        esem = {e: es.enter_context(nc.semaphore("es_" + e)) for e in self.ENGS}
        for i, h in enumerate(self.holders):
            h.sem = es.enter_context(nc.semaphore("hs%d" % i))
        for e in self.ENGS:
            c = 0
            for o in self.ops[e]:
                if o.marked:
                    c += 1
                    o.val = c
        block = es.enter_context(nc.Block())
        prog = self

        def run(e, engobj):
            waited = {}
            for o in prog.ops[e]:
                for de, d in o.w_eng.items():
                    s = esem[de]
                    if waited.get(de, 0) < d.val:
                        engobj.wait_ge(s, d.val)
                        waited[de] = d.val
                for h, v in o.w_buf.items():
                    if waited.get(h, 0) < v:
                        engobj.wait_ge(h.sem, v)
                        waited[h] = v
                ins = o.fn(engobj)
                if o.is_dma and o.cc:
                    ins.then_inc(o.holder.sem)
                elif o.is_dma:
                    ins.then_inc(o.holder.sem, 16)
                elif o.marked:
                    ins.then_inc(esem[e], 1)
            if e == "sp":
                for h in prog.holders:
                    if waited.get(h, 0) < h.dma_count:
                        engobj.wait_ge(h.sem, h.dma_count)
                for e2 in prog.ENGS:
                    if e2 == e:
                        continue
                    last = None
                    for o in prog.ops[e2]:
                        if o.marked:
                            last = o
                    if last is not None and waited.get(e2, 0) < last.val:
                        engobj.wait_ge(esem[e2], last.val)

        @block.tensor
        def _(t):
            run("pe", t)

        @block.scalar
        def _(t):
            run("act", t)

        @block.vector
        def _(t):
            run("dve", t)

        @block.gpsimd
        def _(t):
            run("pool", t)

        @block.sync
        def _(t):
            run("sp", t)


PAGE = 1024


class Arena:
    def __init__(self, nc, es, nbytes):
        self.n = nbytes
        self.t = es.enter_context(nc.sbuf_tensor("arena", [128, nbytes // 2], BF16))
        self.pages = [Buf("pg%d" % i) for i in range(nbytes // PAGE)]
        self.ptr = 0

    def mark(self):
        return self.ptr

    def release(self, m):
        self.ptr = m

    def tile(self, free, dt, parts=128):
        sz = 4 if dt == F32 else 2
        n = int(np.prod(free)) * sz
        npg = (n + PAGE - 1) // PAGE
        off = self.ptr
        self.ptr += npg * PAGE
        assert self.ptr <= self.n, "arena overflow %d" % self.ptr
        ap = self.t[:, off // 2: off // 2 + n // 2]
        if dt == F32:
            ap = ap.bitcast(F32)
        if len(free) == 2:
            ap = ap.rearrange("p (a b) -> p a b", b=free[1])
        elif len(free) == 3:
            ap = ap.rearrange("p (a b c) -> p a b c", b=free[1], c=free[2])
        if parts < 128:
            ap = ap[0:parts]
        return V(ap, self.pages[off // PAGE: off // PAGE + npg])


def _bf(x):
    return np.asarray(x, np.float32).astype(ml_dtypes.bfloat16)


def _dft_tables(L):
    k = np.arange(L, dtype=np.int64)
    t = np.arange(L, dtype=np.int64)
    m = (np.outer(t, k) % (2 * L)).astype(np.float64)
    ang = np.pi * m / L
    fC = np.cos(ang)
    fS = -np.sin(ang)
    fS[:, 0] = np.where(t % 2 == 0, 1.0, -1.0)
    iC = np.cos(ang.T) / L
    iS = -np.sin(ang.T) / L
    iC[0, :] = 1.0 / (2 * L)
    iS[0, :] = np.where(t % 2 == 0, 1.0, -1.0) / (2 * L)
    n = L // 128

    def arr(M):
        return np.ascontiguousarray(M.reshape(n, 128, n, 128).transpose(2, 1, 0, 3))

    return _bf(arr(fC)), _bf(arr(fS)), _bf(arr(iC)), _bf(arr(iS))


def _consts():
    c = {}
    c["ident"] = np.eye(128, dtype=np.float32)
    perm = np.zeros((128, 128), np.float32)
    for m in range(128):
        s, r = divmod(m, 64)
        a, r2 = divmod(r, 32)
        p, f = divmod(r2, 16)
        k = s * 64 + a * 32 + (1 - p) * 16 + f
        perm[k, m] = 1.0
    c["perm"] = perm
    inv = 10000.0 ** (-np.arange(16, dtype=np.float32) / 16)
    tt = np.arange(LS)
    pos = np.stack([tt // 64, tt % 64], -1).astype(np.float32)
    ang = pos[:, :, None] * inv
    cs, sn = np.cos(ang), np.sin(ang)
    cosT = np.zeros((128, LS), np.float32)
    sinT = np.zeros((128, LS), np.float32)
    for m in range(128):
        s, r = divmod(m, 64)
        a, r2 = divmod(r, 32)
        p, f = divmod(r2, 16)
        cosT[m] = cs[:, a, f]
        sinT[m] = (-sn[:, a, f]) if p == 0 else sn[:, a, f]
    c["ropec"] = cosT
    c["ropes"] = sinT
    for L, nm in ((LP, "p"), (LS, "s")):
        fC, fS, iC, iS = _dft_tables(L)
        c["fC" + nm], c["fS" + nm], c["iC" + nm], c["iS" + nm] = fC, fS, iC, iS
        t = np.linspace(0.0, 1.0, L, dtype=np.float32)[:, None]
        w = (2.0 * np.float32(math.pi) * np.arange(L, dtype=np.float32)[:, None] / np.float32(L)).astype(np.float32)
        f = np.linspace(1e-4, 7, 8, dtype=np.float32)[None]
        z = np.concatenate([t, np.cos(f * w), -np.sin(f * w)], -1).astype(np.float32)
        c["zT" + nm] = np.ascontiguousarray(z.T)
        c["negt" + nm] = np.ascontiguousarray((-t[:, 0]).reshape(L // 128, 128).T)
    return c


def _prm_rows(inp, l):
    rows = []
    idx = {}

    def add(name, arr):
        a = np.asarray(arr, np.float32).reshape(-1, 128)
        idx[name] = len(rows)
        rows.extend(list(a))

    def add64(name, arr):
        a = np.zeros(128, np.float32)
        a[:64] = arr
        idx[name] = len(rows)
        rows.append(a)

    add("norm_mix", inp["norm_mix"][l])
    add("norm_ffn", inp["norm_ffn"][l])
    add("final_norm", inp["final_norm"])
    add("subln", inp["att_subln"][l])
    add("lru_cw", inp["lru_conv_w"][l])
    add("lru_cb", inp["lru_conv_b"][l])
    add("lru_ba", inp["lru_ba"][l])
    add("lru_bx", inp["lru_bx"][l])
    add("lru_lam", inp["lru_lambda"][l])
    add("hy_cw", inp["hy_conv_w"][l])
    add("hy_cb", inp["hy_conv_b"][l])
    add64("hy_freq", inp["hy_freq"][l])
    add64("hy_b1", inp["hy_b1"][l])
    add64("hy_b2", inp["hy_b2"][l])
    R = len(rows)
    RP = ((R + 127) // 128) * 128
    out = np.zeros((RP, 128), np.float32)
    out[:R] = np.stack(rows)
    return out, idx


def _roll_inputs(inp, c):
    o = dict(inp)
    sh = -c * 128
    for k in ("lru_conv_w", "lru_conv_b", "lru_ba", "lru_bx", "lru_lambda"):
        o[k] = np.roll(inp[k], sh, axis=-1)
    o["hy_conv_w"] = np.roll(inp["hy_conv_w"].reshape(LAYERS, 3, 3, 1024), sh, axis=-1).reshape(LAYERS, 3, 3072)
    o["hy_conv_b"] = np.roll(inp["hy_conv_b"].reshape(LAYERS, 3, 1024), sh, axis=-1).reshape(LAYERS, 3072)
    return o


class K:
    pass


NV = 3
TO = 512


def build(prm_idx, RP):
    nc = bass.Bass("TRN2", target_bir_lowering=False)
    P = Prog(nc)
    es = ExitStack()
    g = K()

    def din(name, shape, dt=F32):
        return nc.dram_tensor(name, list(shape), dt, kind="ExternalInput").ap()

    def dout(name, shape, dt=F32):
        return nc.dram_tensor(name, list(shape), dt, kind="ExternalOutput").ap()

    def dscr(name, shape, dt):
        return nc.dram_tensor(name, list(shape), dt).ap()

    xin = {"P": din("xp", [NPS * LP, D]), "O": din("xo", [TO, D])}
    cv_in = din("cvecs", [NV, D])
    ck_in = din("ck", [2, LAYERS, PAST, 128])
    cvv_in = din("cv", [2, LAYERS, PAST, 128])
    s0_in = din("s0", [2, LAYERS, 2, 128])
    w_mod = din("w_mod_s", [LAYERS, D, 1536])
    b_mod = din("b_mod_s", [LAYERS, 1536])
    w_in = din("w_in", [LAYERS, D, 8192])
    w_in_s = din("w_in_s", [LAYERS, D, 1024])
    w_gate = din("w_gate", [LAYERS, D, 3 * D])
    att_lambda = din("att_lambda", [LAYERS, 4, 64])
    lru_wa = din("lru_wa", [LAYERS, 2, 16, 64, 64])
    lru_wx = din("lru_wx", [LAYERS, 2, 16, 64, 64])
    lru_wa_s = din("lru_wa_s", [LAYERS, 2, 2, 64, 64])
    lru_wx_s = din("lru_wx_s", [LAYERS, 2, 2, 64, 64])
    hy_w1 = din("hy_w1", [LAYERS, 17, 64])
    hy_w2 = din("hy_w2", [LAYERS, 64, 64])
    hyw = {"P": dict(w3=din("hy_w3", [LAYERS, 64, 4096]), b3=din("hy_b3", [LAYERS, 4096]),
                     dec=din("hy_decay", [LAYERS, 4096]), d=din("hy_d", [LAYERS, 2, 1024]), cs=1024),
           "S": dict(w3=din("hy_w3_s", [LAYERS, 64, 512]), b3=din("hy_b3_s", [LAYERS, 512]),
                     dec=din("hy_decay_s", [LAYERS, 512]), d=din("hy_d_s", [LAYERS, 2, 128]), cs=128)}
    w_br = din("w_br", [LAYERS, 3, 1024, D])
    w_out = din("w_out", [LAYERS, D, D])
    w_ffg = din("w_ff_gate", [LAYERS, D, DFF])
    w_ffu = din("w_ff_up", [LAYERS, D, DFF])
    w_ffd = din("w_ff_down", [LAYERS, DFF, D])
    prm_in = {"P": din("prm", [LAYERS, RP, 128]), "S": din("prm_s", [LAYERS, RP, 128])}
    ident_in = din("ident", [128, 128])
    perm_in = din("perm", [128, 128])
    ropec_in = din("ropec", [128, LS])
    ropes_in = din("ropes", [128, LS])
    sel_in = din("sel", [128, 16, 256], BF16)
    tabs = {}
    for nm, L in (("p", LP), ("s", LS)):
        n = L // 128
        for t4 in ("fC", "fS", "iC", "iS"):
            tabs[t4 + nm] = din(t4 + nm, [n, 128, n, 128], BF16)
        tabs["zT" + nm] = din("zT" + nm, [17, L])
        tabs["negt" + nm] = din("negt" + nm, [128, n])
    yout = {"P": dout("yp", [NPS * LP, D]), "O": dout("yo", [TO, D])}
    nck = dout("nck", [NPS, LAYERS, LP, NH, 128])
    ncv = dout("ncv", [NPS, LAYERS, LP, NH, 128])
    nst = dout("nst", [NPS, LAYERS, 2, 1024])
    Bout = Buf("outs")

    sets = {
        "P": dict(name="P", nseq=NPS, L=LP, T=NPS * LP, rope=False, ctx=False, tb="p", NB=2,
                  vl=[(0, 512, 0)]),
        "S": dict(name="S", nseq=2, L=LS, T=2 * LS, rope=True, ctx=True, tb="s", NB=8),
        "O": dict(name="O", T=TO, NB=1, vl=[(0, 256, 1), (256, 512, 2)]),
    }
    for nm in ("P", "O"):
        s = sets[nm]
        T = s["T"]
        s["xT"] = dscr("xT" + nm, [D, T], F32)
        s["BxT"] = [Buf("xT%s%d" % (nm, i)) for i in range(T // 512)]
        s["hT"] = dscr("hT" + nm, [T // 512, 128, KC, 512], BF16)
        s["BhT"] = [Buf("hT%s%d" % (nm, i)) for i in range(T // 512)]
    sP, sS, sO = sets["P"], sets["S"], sets["O"]
    sP["br"] = dscr("brP", [24, 128, sP["T"]], BF16)
    sP["Bbr"] = Buf("brP")
    sS["hT"] = dscr("hTS", [8, 128, KC, 512], BF16)
    sS["BhT"] = [Buf("hTSall")] * 8
    gin = dscr("gin", [3, 32, 128, 128], BF16)
    Bgin = Buf("gin")
    gout = dscr("gout", [8, 3, 32, 128, 128], BF16)
    Bgout = Buf("gout")
    for nm in ("P", "S"):
        s = sets[nm]
        s["hyt"] = dscr("hyt" + nm, [3, s["nseq"], s["L"] // 128, 128, 1024 if nm == "P" else 128], F32)
        s["Bhyt"] = Buf("hyt" + nm)
    mixd = dscr("mixd", [KC, 128, 1536], BF16)
    Bmixd = Buf("mixd")

    def sb(name, shape, dt):
        t = es.enter_context(nc.sbuf_tensor(name, list(shape), dt))
        return V(t[:], [Buf(name)])

    ident = sb("ident_sb", [128, 128], F32)
    perm = sb("perm_sb", [128, 128], F32)
    ones_bf = sb("ones_bf", [128, 128], BF16)
    onesr = sb("onesr", [1, NV], F32)
    prm = {"P": [sb("prm%d" % l, [128, RP], F32) for l in range(LAYERS)],
           "S": [sb("prms%d" % l, [128, RP], F32) for l in range(LAYERS)]}
    modv = sb("modv", [128, LAYERS, 96, NV], F32)
    modA = sb("modA", [128, LAYERS, 2, NV, KC], F32)
    scal = sb("scal", [128, 64], F32)
    lruw = {"P": sb("lruw", [128, 32, 128], BF16), "S": sb("lruws", [128, 4, 128], BF16)}
    lruc = {"P": sb("lruc", [128, 16, 2], F32), "S": sb("lrucs", [128, 16, 2], F32)}
    lam_sb = sb("lam_sb", [128, 256], F32)
    hyp = sb("hyp", [128, 8], F32)
    hyw1 = sb("hyw1", [17, 64], F32)
    hyw2 = sb("hyw2", [64, 64], F32)
    hdn = {"p": sb("hdn_p", [65, LP], F32), "s": sb("hdn_s", [65, LS], F32)}
    st_sb = sb("st_sb", [128, 64], F32)
    csil = sb("csil", [128, KC, NV], F32)

    psum_t = es.enter_context(nc.psum_tensor("psum", [128, 4096], F32))
    Bps = [Buf("ps%d" % i) for i in range(8)]

    def bank(i, n=512, parts=128):
        return V(psum_t[0:parts, i * 512: i * 512 + n], [Bps[i]])

    A = Arena(nc, es, 168 * 1024)

    def rb(*vs):
        out = []
        for v in vs:
            if isinstance(v, V):
                out.extend(v.bufs)
        return out

    def ap_(v):
        return v.ap if isinstance(v, V) else v

    def mm(out, lhsT, rhs, start, stop):
        P.op("pe", lambda e: e.matmul(out.ap, lhsT=lhsT.ap, rhs=rhs.ap, start=start, stop=stop),
             reads=rb(lhsT, rhs), writes=out.bufs)

    def tr(out, in_):
        P.op("pe", lambda e: e.transpose(out.ap, in_.ap, ident.ap[0:in_.ap.shape[0], 0:in_.ap.shape[0]]),
             reads=rb(in_, ident), writes=out.bufs)

    def act(out, in_, func, scale=1.0, bias=0.0, eng="act"):
        P.op(eng, lambda e: e.activation(out=out.ap, in_=in_.ap, func=func, scale=ap_(scale), bias=ap_(bias)),
             reads=rb(in_, scale, bias), writes=out.bufs)

    def tt(out, in0, in1, op, eng="dve"):
        P.op(eng, lambda e: e.tensor_tensor(out=out.ap, in0=in0.ap, in1=in1.ap, op=op),
             reads=rb(in0, in1), writes=out.bufs)

    def ts(out, in0, s1, s2, op0, op1=None, eng="dve"):
        if op1 is None:
            P.op(eng, lambda e: e.tensor_scalar(out=out.ap, in0=in0.ap, scalar1=ap_(s1), scalar2=None, op0=op0),
                 reads=rb(in0, s1), writes=out.bufs)
        else:
            P.op(eng, lambda e: e.tensor_scalar(out=out.ap, in0=in0.ap, scalar1=ap_(s1), scalar2=ap_(s2), op0=op0, op1=op1),
                 reads=rb(in0, s1, s2), writes=out.bufs)

    def stt(out, in0, scalar, in1, op0, op1):
        P.op("dve", lambda e: e.scalar_tensor_tensor(out=out.ap, in0=in0.ap, scalar=ap_(scalar), in1=in1.ap, op0=op0, op1=op1),
             reads=rb(in0, scalar, in1), writes=out.bufs)

    def cp(out, in_, eng="dve"):
        if eng == "act":
            act(out, in_, AF.Copy)
        else:
            P.op(eng, lambda e: e.tensor_copy(out=out.ap, in_=in_.ap), reads=rb(in_), writes=out.bufs)

    def memset(out, val, eng="dve"):
        P.op(eng, lambda e: e.memset(out.ap, val), writes=out.bufs)

    def scan(out, d0, d1, init):
        P.op("dve", lambda e: e.tensor_tensor_scan(out=out.ap, data0=d0.ap, data1=d1.ap, initial=ap_(init),
                                                   op0=ALU.mult, op1=ALU.add),
             reads=rb(d0, d1, init), writes=out.bufs)

    def recip(out, in_):
        P.op("dve", lambda e: e.reciprocal(out=out.ap, in_=in_.ap), reads=rb(in_), writes=out.bufs)

    NHOLD = 16
    hold = {"sp": [Buf("hsp%d" % i) for i in range(NHOLD)], "pool": [Buf("hpl%d" % i) for i in range(NHOLD)]}
    hctr = {"sp": 0, "pool": 0}

    def dma(out_ap, in_ap, reads, writes, holder=None, q="sp", part=False):
        h = hold[q][hctr[q] % NHOLD]
        hctr[q] += 1
        P.op(q, lambda e: e.dma_start(out=out_ap, in_=in_ap), reads=reads, writes=writes, dma=True, holder=h, part=part)

    def ld(dst, src_ap, q="sp", extra_reads=(), part=False):
        dma(dst.ap, src_ap, list(extra_reads), dst.bufs, None, q, part)

    def st(dst_ap, src, dbufs, holder=None):
        dma(dst_ap, src.ap, src.bufs, dbufs, None, "sp", True)

    def wload(dst, src_ap, part=False):
        ld(dst, src_ap, q="pool", part=part)

    def allgather(src_ap, dst_ap, rbufs, wbufs, name):
        h = Buf("cc_" + name)
        P.op("pool", lambda e: e.collective_compute("AllGather", ALU.bypass, replica_groups=[list(range(NCORES))],
                                                    ins=[src_ap.opt()], outs=[dst_ap.opt()]),
             reads=rbufs, writes=wbufs, dma=True, holder=h, cc=True)

    ld(ident, ident_in)
    ld(perm, perm_in)
    memset(ones_bf, 1.0)
    memset(onesr, 1.0)
    memset(st_sb, 0.0)
    for kind in ("P", "S"):
        for l in range(LAYERS):
            for c0 in range(RP // 128):
                m0 = A.mark()
                t0 = A.tile([128], F32)
                ld(t0, prm_in[kind][l, c0 * 128:(c0 + 1) * 128, :])
                pb = bank(c0 % 2, 128)
                tr(pb, t0)
                cp(prm[kind][l][:, c0 * 128:(c0 + 1) * 128], pb)
                A.release(m0)

    def pcol(l, name, r=0, n=1, kind="P"):
        c = prm_idx[name] + r
        return prm[kind][l][:, c:c + n]

    m0 = A.mark()
    ctok = A.tile([D], F32, parts=NV)
    ld(ctok, cv_in)
    for kc in range(KC):
        pb = bank(kc % 2, NV)
        tr(pb, ctok[:, kc * 128:(kc + 1) * 128])
        act(csil[:, kc, :], pb, AF.Silu)
    A.release(m0)
    mod_in = dscr("mod_in", [128, LAYERS * 12 * NV], F32)
    mod_out = dscr("mod_out", [NCORES, 128, LAYERS * 12 * NV], F32)
    Bmod_in, Bmod_out = Buf("mod_in"), Buf("mod_out")
    m0 = A.mark()
    mstage = A.tile([LAYERS, 12, NV], F32)
    for l in range(LAYERS):
        bmod_sb = A.tile([1536], F32, parts=1)
        ld(bmod_sb, b_mod[l:l + 1, :])
        wb = [A.tile([KC, 512], F32) for _ in range(2)]
        for ct in range(3):
            w = wb[ct % 2]
            ld(w, w_mod[l, :, ct * 512:(ct + 1) * 512].rearrange("(k p) n -> p k n", p=128))
            for mc in range(4):
                col = ct * 4 + mc
                pb = bank(2 + (col % 2), NV)
                for kc in range(KC):
                    mm(pb, w[:, kc, mc * 128:(mc + 1) * 128], csil[:, kc, :], kc == 0, False)
                mm(pb, bmod_sb[:, col * 128:(col + 1) * 128], onesr, False, True)
                cp(mstage[:, l, col, :], pb)
    st(mod_in, mstage.rr("p a b c -> p (a b c)"), [Bmod_in])
    A.release(m0)
    allgather(mod_in, mod_out, [Bmod_in], [Bmod_out], "mod")
    for r in range(NCORES):
        dma(modv.ap[:, :, r * 12:(r + 1) * 12, :], mod_out[r].rearrange("p (a b c) -> p a b c", a=LAYERS, b=12),
            [Bmod_out], modv.bufs, None, "sp", True)
    for l in range(LAYERS):
        for wi, (gn, j) in enumerate((("norm_mix", 1), ("norm_ffn", 4))):
            for v in range(NV):
                ts(modA[:, l, wi, v, :], modv[:, l, j * 16:(j + 1) * 16, v], 1.0, None, ALU.add)
                tt(modA[:, l, wi, v, :], modA[:, l, wi, v, :], pcol(l, gn, 0, KC), ALU.mult)

    def modcol(l, j, kc, v):
        return modv[:, l, j * 16 + kc, v:v + 1]

    eps_n = scal[:, 0:1]
    eps_s = scal[:, 1:2]
    memset(scal[:, 0:1], 1e-6)
    memset(scal[:, 1:2], 1e-5)

    def norm_block(xb, l, wi, vl, dst):
        m0 = A.mark()
        sq = A.tile([KC, 512], BF16)
        act(sq, xb, AF.Square)
        pb = bank(7)
        for kc in range(KC):
            mm(pb, ones_bf, sq[:, kc, :], kc == 0, kc == KC - 1)
        rstd = A.tile([512], F32)
        act(rstd, pb, AF.Ln, scale=1.0 / D, bias=eps_n)
        act(rstd, rstd, AF.Exp, scale=-0.5)
        tmp = A.tile([KC, 512], F32)
        tt(tmp, xb, rstd.rr("p (o n) -> p o n", o=1).bc([128, KC, 512]), ALU.mult)
        for kc in range(KC):
            if wi == 2:
                ts(dst[:, kc, :], tmp[:, kc, :], pcol(0, "final_norm", kc), None, ALU.mult)
            else:
                for (c0, c1, v) in vl:
                    act(dst[:, kc, c0:c1], tmp[:, kc, c0:c1], AF.Identity, scale=modA[:, l, wi, v, kc:kc + 1],
                        bias=modcol(l, 0 if wi == 0 else 3, kc, v))
        A.release(m0)

    def load_xT_layer0(s):
        T = s["T"]
        for b in range(T // 512):
            m0 = A.mark()
            stg = A.tile([KC, 512], F32)
            for q in range(4):
                xt = A.tile([D], F32)
                ld(xt, xin[s["name"]][b * 512 + q * 128: b * 512 + (q + 1) * 128, :])
                for k4 in range(4):
                    pb = bank(k4 % 2)
                    for i in range(4):
                        kc = k4 * 4 + i
                        tr(pb[:, i * 128:(i + 1) * 128], xt[:, kc * 128:(kc + 1) * 128])
                    cp(stg[:, k4 * 4:(k4 + 1) * 4, q * 128:(q + 1) * 128],
                       pb.rr("p (i n) -> p i n", i=4), eng=("act" if k4 % 2 else "dve"))
            st(s["xT"][:, b * 512:(b + 1) * 512].rearrange("(k p) n -> p k n", p=128), stg, [s["BxT"][b]])
            A.release(m0)

    def norm_phase(s, l, wi, blocks, dsts, store_h):
        for b, dst in zip(blocks, dsts):
            m0 = A.mark()
            xb = A.tile([KC, 512], F32)
            ld(xb, s["xT"][:, b * 512:(b + 1) * 512].rearrange("(k p) n -> p k n", p=128), extra_reads=[s["BxT"][b]])
            norm_block(xb, l, wi, s["vl"], dst)
            A.release(m0)
            if store_h:
                st(s["hT"][b], dst, [s["BhT"][b]])

    def layer_setup(l):
        ld(lam_sb, att_lambda[l].rearrange("a b -> (a b)").partition_broadcast(128))
        tt(lam_sb[:, 0:64], lam_sb[:, 0:64], lam_sb[:, 64:128], ALU.mult)
        tt(lam_sb[:, 128:192], lam_sb[:, 128:192], lam_sb[:, 192:256], ALU.mult)
        P.op("dve", lambda e: e.reduce_sum(out=scal.ap[:, 2:3], in_=lam_sb.ap[:, 0:64], axis=mybir.AxisListType.X),
             reads=lam_sb.bufs, writes=scal.bufs)
        P.op("dve", lambda e: e.reduce_sum(out=scal.ap[:, 3:4], in_=lam_sb.ap[:, 128:192], axis=mybir.AxisListType.X),
             reads=lam_sb.bufs, writes=scal.bufs)
        act(scal[:, 2:4], scal[:, 2:4], AF.Exp)
        tt(scal[:, 4:5], scal[:, 3:4], scal[:, 2:3], ALU.subtract)
        ts(scal[:, 4:5], scal[:, 4:5], -LAM_INIT[l], None, ALU.add)
        ts(scal[:, 5:6], pcol(l, "subln"), 1.0 - LAM_INIT[l], None, ALU.mult)
        for kind in ("P", "S"):
            lamc = pcol(l, "lru_lam", 0, 16, kind)
            m0 = A.mark()
            t1 = A.tile([16], F32)
            act(t1, lamc, AF.Exp, scale=-1.0)
            act(t1, t1, AF.Ln, bias=1.0)
            ts(lruc[kind][:, :, 0], t1, -8.0, None, ALU.mult)
            ts(lruc[kind][:, :, 1], t1, -16.0, None, ALU.mult)
            A.release(m0)
            memset(lruw[kind], 0.0)
        for dr in range(2):
            for wi, (wsrc, wsrc_s) in enumerate(((lru_wa, lru_wa_s), (lru_wx, lru_wx_s))):
                for hb in range(2):
                    dma(lruw["S"].ap[hb * 64:(hb + 1) * 64, dr * 2 + wi, hb * 64:(hb + 1) * 64], wsrc_s[l, dr, hb],
                        [], lruw["S"].bufs, None, "pool", True)
                for gg in range(8):
                    slot = (dr * 2 + wi) * 8 + gg
                    for hb in range(2):
                        dma(lruw["P"].ap[hb * 64:(hb + 1) * 64, slot, hb * 64:(hb + 1) * 64], wsrc[l, dr, 2 * gg + hb],
                            [], lruw["P"].bufs, None, "pool", True)
        ts(hyp[:, 0:1], pcol(l, "hy_freq"), 1.0 / 3.0, None, ALU.mult)
        tt(hyp[:, 1:2], hyp[:, 0:1], pcol(l, "hy_b1"), ALU.mult)
        tt(hyp[:, 2:3], hyp[:, 0:1], pcol(l, "hy_b2"), ALU.mult)
        ld(hyw1, hy_w1[l])
        ld(hyw2, hy_w2[l])
        for nm, L in (("p", LP), ("s", LS)):
            hd = hdn[nm]
            for b0 in range(0, L, 512):
                n = min(512, L - b0)
                m0 = A.mark()
                zt = A.tile([n], F32, parts=17)
                ld(zt, tabs["zT" + nm][:, b0:b0 + n])
                h1 = A.tile([n], F32, parts=64)
                sq = A.tile([n], F32, parts=64)
                for li in range(2):
                    pb = bank(0, n, 64)
                    if li == 0:
                        mm(pb, hyw1, zt, True, True)
                    else:
                        mm(pb, hyw2, h1, True, True)
                    dst = h1 if li == 0 else hd[0:64, b0:b0 + n]
                    s3 = A.tile([n], F32, parts=64)
                    act(s3, pb, AF.Sin, scale=hyp[0:64, 0:1], bias=hyp[0:64, 1 + li:2 + li])
                    tt(sq, s3, s3, ALU.mult)
                    ts(sq, sq, -4.0, 3.0, ALU.mult, ALU.add)
                    tt(dst, s3, sq, ALU.mult)
                A.release(m0)
            memset(hd[64:65, :], 1.0)

    def project(l, s, col_list, dsts, hts):
        kind = s["name"]
        ws = []
        for cap in col_list:
            w = g.wpool[g.wctr % len(g.wpool)]
            g.wctr += 1
            wload(w, cap.rearrange("(k p) n -> p k n", p=128))
            ws.append(w)
        for b in range(s["NB"]):
            if hts is not None:
                hb = hts[b]
            else:
                hb = g.hstream[g.hctr % 2]
                g.hctr += 1
                ld(hb, s["hT"][b], extra_reads=[s["BhT"][b]])
            for ci, w in enumerate(ws):
                pb = bank((b * len(ws) + ci) % 2)
                for kc in range(KC):
                    mm(pb, w[:, kc, :], hb[:, kc, :], kc == 0, kc == KC - 1)
                if kind == "P":
                    cp(dsts[ci].rr("p a b -> p (a b)")[:, b * 512:(b + 1) * 512], pb, eng="act")
                else:
                    cp(dsts[ci][:, :, b * 256:(b + 1) * 256], pb.rr("p (q t) -> p q t", q=2), eng="act")

    def mixers(s, l, hts):
        T, L, nseq, NB = s["T"], s["L"], s["nseq"], s["NB"]
        kind = s["name"]
        isS = kind == "S"
        TC = L // 128
        nkc = TC + (2 if s["ctx"] else 0)
        nq = min(L, 512)
        neglam = scal[:, 4:5]
        gsub = scal[:, 5:6]
        g.wpool = [A.tile([KC, 128], BF16) for _ in range(5)]
        g.wctr = 0
        if isS:
            g.hstream = [A.tile([KC, 512], BF16) for _ in range(2)]
            g.hctr = 0

        def wcol(off, gI, ci):
            if isS:
                return w_in_s[l, :, ci * 128:(ci + 1) * 128]
            return w_in[l, :, off + gI * 128: off + (gI + 1) * 128]

        def emit_branch_S(n, x32, sq_, t0, ncols):
            nch = ncols // 128
            pb = bank(6, ncols)
            for i in range(nch):
                tr(pb[:, i * 128:(i + 1) * 128], x32[:, i * 128:(i + 1) * 128])
            m9 = A.mark()
            stg = A.tile([nch, 128], BF16)
            cp(stg, pb.rr("p (a n) -> p a n", a=nch), eng="act")
            c0 = sq_ * 16 + t0 // 128
            st(gin[n, c0:c0 + nch].rearrange("c p f -> p c f"), stg, [Bgin])
            A.release(m9)

        for gI in range(1 if isS else 8):
            m0 = A.mark()
            q32 = A.tile([nseq, L], F32)
            k32 = A.tile([nseq, L], F32)
            v32 = A.tile([nseq, L], F32)
            project(l, s, [wcol(0, gI, 0), wcol(1024, gI, 1), wcol(2048, gI, 2)], [q32, k32, v32], hts)
            qb = A.tile([nseq, L], BF16)
            kb = A.tile([nseq, nkc * 128], BF16)
            vtok = A.tile([nseq, nkc, 128], BF16)
            if s["rope"]:
                m1 = A.mark()
                rc = A.tile([LS], F32)
                rs = A.tile([LS], F32)
                ld(rc, ropec_in)
                ld(rs, ropes_in)
                for src, dstb in ((q32, qb), (k32, kb)):
                    for sq_ in range(nseq):
                        for b in range(L // 512):
                            sl = slice(b * 512, (b + 1) * 512)
                            pb = bank(2 + b % 2)
                            mm(pb, perm, src[:, sq_, sl], True, True)
                            m3 = A.mark()
                            t1 = A.tile([512], F32)
                            t2 = A.tile([512], F32)
                            tt(t1, pb, rs[:, sl], ALU.mult)
                            tt(t2, src[:, sq_, sl], rc[:, sl], ALU.mult)
                            tt(dstb[:, sq_, sl], t1, t2, ALU.add)
                            A.release(m3)
                A.release(m1)
            else:
                cp(qb, q32)
                cp(kb[:, :, 0:L], k32)
            for sq_ in range(nseq):
                for c in range(TC):
                    pb = bank(2 + c % 2)
                    tr(pb[:, 0:128], v32[:, sq_, c * 128:(c + 1) * 128])
                    cp(vtok[:, sq_, c, :], pb[:, 0:128])
                    if not s["ctx"]:
                        tr(pb[:, 128:256], k32[:, sq_, c * 128:(c + 1) * 128])
                        stg = A.tile([2, 128], F32)
                        cp(stg, pb[:, 0:256].rr("p (a n) -> p a n", a=2), eng="act")
                        st(ncv[sq_, l, c * 128:(c + 1) * 128, gI, :], stg[:, 0, :], [Bout])
                        st(nck[sq_, l, c * 128:(c + 1) * 128, gI, :], stg[:, 1, :], [Bout])
            if s["ctx"]:
                for sq_ in range(nseq):
                    ctk = A.tile([2, 128], F32)
                    ctv = A.tile([2, 128], F32)
                    ld(ctk, ck_in[sq_, l].rearrange("(c p) n -> p c n", p=128))
                    ld(ctv, cvv_in[sq_, l].rearrange("(c p) n -> p c n", p=128))
                    for c in range(2):
                        pb = bank(2 + c)
                        tr(pb[:, 0:128], ctk[:, c, :])
                        cp(kb[:, sq_, (TC + c) * 128:(TC + c + 1) * 128], pb[:, 0:128])
                        cp(vtok[:, sq_, TC + c, :], ctv[:, c, :])
            att_out = None if isS else A.tile([nseq, L], BF16)
            pT = [A.tile([nq], BF16) for _ in range(3)]
            pi = 0
            for sq_ in range(nseq):
                for q0 in range(0, L, nq):
                    qs = slice(q0, q0 + nq)
                    ob = [bank(2, nq), bank(3, nq)]
                    db = [bank(4, nq), bank(5, nq)]
                    for i in range(2):
                        hs = slice(64 * i, 64 * i + 64)
                        for c in range(nkc):
                            sb_ = bank(c % 2, nq)
                            mm(sb_, kb[hs, sq_, c * 128:(c + 1) * 128], qb[hs, sq_, qs], True, True)
                            pt = pT[pi % 3]
                            pi += 1
                            act(pt, sb_, AF.Exp, scale=0.125)
                            mm(ob[i], vtok[:, sq_, c, :], pt, c == 0, c == nkc - 1)
                            mm(db[i], ones_bf, pt, c == 0, c == nkc - 1)
                    m2 = A.mark()
                    r0 = A.tile([nq], F32)
                    r1 = A.tile([nq], F32)
                    a0 = A.tile([nq], F32)
                    a1 = A.tile([nq], F32)
                    recip(r0, db[0])
                    recip(r1, db[1])
                    tt(a0, ob[0], r0, ALU.mult)
                    tt(a1, ob[1], r1, ALU.mult)
                    stt(a0, a1, neglam, a0, ALU.mult, ALU.add)
                    sqv = A.tile([nq], BF16)
                    act(sqv, a0, AF.Square)
                    mb = bank(6, nq)
                    mm(mb, ones_bf, sqv, True, True)
                    act(r0, mb, AF.Ln, scale=1.0 / 128, bias=eps_s)
                    act(r0, r0, AF.Exp, scale=-0.5)
                    if isS:
                        stt(a1, a0, gsub, r0, ALU.mult, ALU.mult)
                        emit_branch_S(0, a1, sq_, q0, nq)
                    else:
                        stt(att_out[:, sq_, qs], a0, gsub, r0, ALU.mult, ALU.mult)
                    A.release(m2)
            if not isS:
                st(s["br"][0 * 8 + gI], att_out.rr("p a b -> p (a b)"), [s["Bbr"]])
            A.release(m0)

            m0 = A.mark()
            xl = A.tile([nseq, L], F32)
            gl = A.tile([nseq, L], F32)
            xlf = xl.rr("p a b -> p (a b)")
            glf = gl.rr("p a b -> p (a b)")
            project(l, s, [wcol(3072, gI, 3), wcol(4096, gI, 4)], [xl, gl], hts)
            xc = A.tile([nseq, L], F32)
            xcf = xc.rr("p a b -> p (a b)")
            cw = lambda k: pcol(l, "lru_cw", k * 8 + gI, 1, kind)
            ts(xc, xl, cw(2), pcol(l, "lru_cb", gI, 1, kind), ALU.mult, ALU.add)
            stt(xc[:, :, 2:L], xl[:, :, 0:L - 2], cw(0), xc[:, :, 2:L], ALU.mult, ALU.add)
            stt(xc[:, :, 1:L], xl[:, :, 0:L - 1], cw(1), xc[:, :, 1:L], ALU.mult, ALU.add)
            stt(xc[:, :, 0:L - 1], xl[:, :, 1:L], cw(3), xc[:, :, 0:L - 1], ALU.mult, ALU.add)
            xcb = A.tile([T], BF16)
            cp(xcb, xcf)
            hsum = xl
            hsumf = xlf
            useq = nseq if nseq * L <= 1024 else 1
            UL = useq * L
            m1 = A.mark()
            lsets = [(A.tile([UL], F32), A.tile([UL], F32), A.tile([UL], F32)) for _ in range(2)]
            ltmps = [(A.tile([512], F32), A.tile([512], F32)) for _ in range(2)]
            ui = 0
            ti = 0
            for dr in range(2):
                wsl = (dr * 2 + 0) if isS else (dr * 2 + 0) * 8 + gI
                wsl2 = (dr * 2 + 1) if isS else (dr * 2 + 1) * 8 + gI
                for u0 in range(0, nseq, useq):
                    av, bv, hv = lsets[ui % 2]
                    ui += 1
                    for t0 in range(0, UL, 512):
                        sl = slice(t0, t0 + 512)
                        fsl = slice(u0 * L + t0, u0 * L + t0 + 512)
                        r, ig = ltmps[ti % 2]
                        pr = bank(2 + 2 * (ti % 2))
                        pi_ = bank(3 + 2 * (ti % 2))
                        ti += 1
                        mm(pr, lruw[kind][:, wsl, :], xcb[:, fsl], True, True)
                        mm(pi_, lruw[kind][:, wsl2, :], xcb[:, fsl], True, True)
                        act(r, pr, AF.Sigmoid, bias=pcol(l, "lru_ba", dr * 8 + gI, 1, kind))
                        act(ig, pi_, AF.Sigmoid, bias=pcol(l, "lru_bx", dr * 8 + gI, 1, kind))
                        act(av[:, sl], r, AF.Exp, scale=lruc[kind][:, dr * 8 + gI, 0:1])
                        act(r, r, AF.Exp, scale=lruc[kind][:, dr * 8 + gI, 1:2])
                        ts(r, r, -1.0, 1.0, ALU.mult, ALU.add)
                        ts(r, r, 0.0, None, ALU.max)
                        act(r, r, AF.Sqrt)
                        tt(ig, ig, xcf[:, fsl], ALU.mult)
                        tt(bv[:, sl], ig, r, ALU.mult)
                    for sq in range(useq):
                        sq_ = u0 + sq
                        lsl = slice(sq * L, (sq + 1) * L)
                        m2 = A.mark()
                        if s["ctx"]:
                            h0 = A.tile([1], F32)
                            h0r = A.tile([128], F32, parts=1)
                            ld(h0r, s0_in[sq_, l, dr:dr + 1, :])
                            pb = bank(6, 1)
                            tr(pb, h0r)
                            cp(h0, pb)
                            init = h0
                        else:
                            init = 0.0
                        if dr == 0:
                            scan(hv[:, lsl], av[:, lsl], bv[:, lsl], init)
                            if not s["ctx"]:
                                cp(st_sb[:, (sq_ * 2 + 0) * 8 + gI:(sq_ * 2 + 0) * 8 + gI + 1], hv[:, (sq + 1) * L - 1:(sq + 1) * L], eng="act")
                            cp(hsum[:, sq_, :], hv[:, lsl], eng="act")
                        else:
                            scan(hv[:, lsl][:, ::-1], av[:, lsl][:, ::-1], bv[:, lsl][:, ::-1], init)
                            if not s["ctx"]:
                                cp(st_sb[:, (sq_ * 2 + 1) * 8 + gI:(sq_ * 2 + 1) * 8 + gI + 1], hv[:, sq * L:sq * L + 1], eng="act")
                            tt(hsum[:, sq_, :], hsum[:, sq_, :], hv[:, lsl], ALU.add)
                        A.release(m2)
            A.release(m1)
            gg = A.tile([T], F32)
            act(gg, glf, AF.Gelu_apprx_tanh)
            if isS:
                tt(gg, hsumf, gg, ALU.mult)
                for sq_ in range(nseq):
                    for t0 in range(0, L, 512):
                        emit_branch_S(1, gg[:, sq_ * L + t0: sq_ * L + t0 + 512], sq_, t0, 512)
            else:
                lo = A.tile([T], BF16)
                tt(lo, hsumf, gg, ALU.mult)
                st(s["br"][1 * 8 + gI], lo, [s["Bbr"]])
            A.release(m0)

            for wh in range(3):
                m0 = A.mark()
                hx = A.tile([nseq, L], F32)
                project(l, s, [wcol(5120 + wh * 1024, gI, 5 + wh)], [hx], hts)
                hc = A.tile([nseq, L], F32)
                hw = lambda k: pcol(l, "hy_cw", k * 24 + wh * 8 + gI, 1, kind)
                ts(hc, hx, hw(1), pcol(l, "hy_cb", wh * 8 + gI, 1, kind), ALU.mult, ALU.add)
                stt(hc[:, :, 1:L], hx[:, :, 0:L - 1], hw(0), hc[:, :, 1:L], ALU.mult, ALU.add)
                stt(hc[:, :, 0:L - 1], hx[:, :, 1:L], hw(2), hc[:, :, 0:L - 1], ALU.mult, ALU.add)
                for sq_ in range(nseq):
                    for c4 in range(0, TC, 4):
                        n4 = min(4, TC - c4)
                        pb = bank(2 + (c4 // 4) % 2)
                        for i in range(n4):
                            tr(pb[:, i * 128:(i + 1) * 128], hc[:, sq_, (c4 + i) * 128:(c4 + i + 1) * 128])
                        m9 = A.mark()
                        stg = A.tile([4, 128], F32)
                        cp(stg[:, 0:n4, :], pb[:, 0:n4 * 128].rr("p (a n) -> p a n", a=n4), eng="act")
                        st(s["hyt"][wh, sq_, c4:c4 + n4, :, gI * 128:(gI + 1) * 128].rearrange("c p n -> p c n"),
                           stg[:, 0:n4, :], [s["Bhyt"]])
                        A.release(m9)
                A.release(m0)

    def hyena2(s, l):
        L, nseq = s["L"], s["nseq"]
        kind = s["name"]
        isS = kind == "S"
        TC = L // 128
        nm = s["tb"]
        hw_ = hyw[kind]
        cs_ = hw_["cs"]
        nF = 128
        NCOL = nseq * 128
        m_top = A.mark()
        w3 = A.tile([4 * cs_], F32, parts=65)
        ld(w3[0:64, :], hw_["w3"][l], part=True)
        ld(w3[64:65, :], hw_["b3"][l:l + 1, :], part=True)
        negt = A.tile([TC], F32)
        ld(negt, tabs["negt" + nm])
        for cb in range(1 if isS else 8):
            g0 = cb
            m0 = A.mark()
            dec = A.tile([4, nF], F32)
            for od in range(4):
                ld(dec[:, od, :], hw_["dec"][l, od * cs_ + g0 * 128: od * cs_ + g0 * 128 + nF].partition_broadcast(128), part=True)
            m9 = A.mark()
            tmpd = A.tile([4, nF], F32)
            ts(tmpd, dec, -1.0, None, ALU.mult)
            tt(dec, dec, tmpd, ALU.max)
            A.release(m9)
            dD = A.tile([2, nF], F32)
            for o in range(2):
                ld(dD[:, o, :], hw_["d"][l, o, g0 * 128: g0 * 128 + nF].partition_broadcast(128), part=True)
            z32 = A.tile([TC, NCOL], F32)
            ub = A.tile([TC, NCOL], BF16)

            def colsrc(wh, c):
                return s["hyt"][wh, :, c, :, g0 * 128:(g0 + 1) * 128].rearrange("s p n -> p s n")

            def v3(x):
                return x.rr("p (s n) -> p s n", s=nseq)

            for c in range(TC):
                dma(v3(z32[:, c, :]).ap, colsrc(0, c), [s["Bhyt"]], z32.bufs, None, "sp", True)
            for o in range(2):
                cp(ub, z32)
                m1 = A.mark()
                hsm = A.tile([TC, nF], BF16)
                hdf = A.tile([TC, nF], BF16)
                for c in range(TC):
                    m2 = A.mark()
                    win = A.tile([2, nF], F32)
                    fw = A.tile([2, nF], F32)
                    act(win, dec[:, 2 * o:2 * o + 2, :], AF.Exp, scale=negt[:, c:c + 1])
                    for dr in range(2):
                        pb = bank(dr, nF)
                        cols = (o * 2 + dr) * cs_ + g0 * 128
                        mm(pb, hdn[nm][:, c * 128:(c + 1) * 128], w3[:, cols:cols + 128], True, True)
                        tt(fw[:, dr, :], pb, win[:, dr, :], ALU.mult)
                    if c == 0:
                        memset(fw[0:1, 1, :], 0.0)
                    tt(hsm[:, c, :], fw[:, 0, :], fw[:, 1, :], ALU.add)
                    tt(hdf[:, c, :], fw[:, 0, :], fw[:, 1, :], ALU.subtract)
                    A.release(m2)
                Yr = A.tile([TC, NCOL], BF16)
                Yi = A.tile([TC, NCOL], BF16)
                tb2 = [(A.tile([TC, 128], BF16), A.tile([TC, 128], BF16)) for _ in range(2)]
                for kc in range(TC):
                    tC, tS = tb2[kc % 2]
                    ld(tC, tabs["fC" + nm][kc])
                    ld(tS, tabs["fS" + nm][kc])
                    pFr, pFi, pUr, pUi = bank(0, nF), bank(1, nF), bank(2, NCOL), bank(3, NCOL)
                    if kc % 2:
                        pFr, pFi, pUr, pUi = bank(4, nF), bank(5, nF), bank(6, NCOL), bank(7, NCOL)
                    for c in range(TC):
                        mm(pFr, tC[:, c, :], hsm[:, c, :], c == 0, c == TC - 1)
                    for c in range(TC):
                        mm(pFi, tS[:, c, :], hdf[:, c, :], c == 0, c == TC - 1)
                    for c in range(TC):
                        mm(pUr, tC[:, c, :], ub[:, c, :], c == 0, c == TC - 1)
                    for c in range(TC):
                        mm(pUi, tS[:, c, :], ub[:, c, :], c == 0, c == TC - 1)
                    m2 = A.mark()
                    Fr = A.tile([nF], F32)
                    Fi = A.tile([nF], F32)
                    cp(Fr, pFr, eng="act")
                    cp(Fi, pFi, eng="act")
                    if kc == 0:
                        pN = bank(4, nF, 1)
                        for c in range(TC):
                            mm(pN, tS[:, c, 0:1], hsm[:, c, :], c == 0, c == TC - 1)
                        cp(Fi[0:1, :], pN, eng="act")
                    Frb = Fr.rr("p (o n) -> p o n", o=1).bc([128, nseq, 128])
                    Fib = Fi.rr("p (o n) -> p o n", o=1).bc([128, nseq, 128])
                    t1 = A.tile([NCOL], F32)
                    t2 = A.tile([NCOL], F32)
                    tt(v3(t1), v3(pUr), Frb, ALU.mult)
                    tt(v3(t2), v3(pUi), Fib, ALU.mult)
                    tt(Yr[:, kc, :], t1, t2, ALU.subtract)
                    if kc == 0:
                        cp(Yr[0:1, kc, :], t1[0:1, :])
                    tt(v3(t1), v3(pUr), Fib, ALU.mult)
                    tt(v3(t2), v3(pUi), Frb, ALU.mult)
                    if kc == 0:
                        tt(v3(t1)[0:1], v3(pUi)[0:1], Fib[0:1], ALU.mult)
                        memset(t2[0:1, :], 0.0)
                    tt(Yi[:, kc, :], t1, t2, ALU.add)
                    A.release(m2)
                for tcn in range(TC):
                    tC, tS = tb2[tcn % 2]
                    ld(tC, tabs["iC" + nm][tcn])
                    ld(tS, tabs["iS" + nm][tcn])
                    pb = bank(tcn % 2, NCOL)
                    for kc in range(TC):
                        mm(pb, tC[:, kc, :], Yr[:, kc, :], kc == 0, False)
                    for kc in range(TC):
                        mm(pb, tS[:, kc, :], Yi[:, kc, :], False, kc == TC - 1)
                    m2 = A.mark()
                    gt = A.tile([NCOL], F32)
                    dma(v3(gt).ap, colsrc(1 + o, tcn), [s["Bhyt"]], gt.bufs, None, "sp")
                    t1 = A.tile([NCOL], F32)
                    tt(v3(t1), v3(z32[:, tcn, :]), dD[:, o:o + 1, :].bc([128, nseq, 128]), ALU.mult)
                    tt(t1, t1, pb, ALU.add)
                    tt(z32[:, tcn, :], t1, gt, ALU.mult)
                    A.release(m2)
                A.release(m1)
            if isS:
                m2 = A.mark()
                zb = A.tile([TC, NCOL], BF16)
                cp(zb, z32)
                for q in range(nseq):
                    st(gin[2, q * 16:(q + 1) * 16].rearrange("c p f -> p c f"), zb[:, :, q * 128:(q + 1) * 128], [Bgin])
                A.release(m2)
            else:
                for c in range(TC):
                    for j in range(nseq):
                        pb = bank(2 + j % 2, 128)
                        tr(pb, z32[:, c, j * 128:(j + 1) * 128])
                        m2 = A.mark()
                        ot = A.tile([128], BF16)
                        cp(ot, pb, eng=("act" if j % 2 else "dve"))
                        st(s["br"][16 + g0, :, j * L + c * 128: j * L + (c + 1) * 128], ot, [s["Bbr"]])
                        A.release(m2)
            A.release(m0)
        A.release(m_top)

    BL = [(sP, 0), (sP, 1), (sO, 0)]
    NBL = len(BL)

    def xT_prefetch(dchunk):
        tiles = g.xtpool[g.xtctr % 2]
        g.xtctr += 1
        for bi, (s, b) in enumerate(BL):
            dma(tiles[bi].ap, s["xT"][dchunk * 128:(dchunk + 1) * 128, b * 512:(b + 1) * 512], [s["BxT"][b]],
                tiles[bi].bufs, None, "sp")
        return tiles

    def xT_update(dchunk, pbs, j, tiles):
        for bi, (s, b) in enumerate(BL):
            xt = tiles[bi]
            for (c0, c1, v) in s["vl"]:
                stt(xt[:, c0:c1], pbs[bi][:, c0:c1], modcol(g.l, j, dchunk, v), xt[:, c0:c1], ALU.mult, ALU.add)
            dma(s["xT"][dchunk * 128:(dchunk + 1) * 128, b * 512:(b + 1) * 512], xt.ap, xt.bufs, [s["BxT"][b]], None, "sp", True)

    def phaseB(l):
        g.l = l
        m0 = A.mark()
        hts = [A.tile([KC, 512], BF16) for _ in range(NBL)]
        for bi, (s, b) in enumerate(BL):
            ld(hts[bi], s["hT"][b], extra_reads=[s["BhT"][b]])
        brT = A.tile([24, 1536], BF16)
        dma(brT.ap[:, :, 0:1024], sP["br"].rearrange("c p n -> p c n"), [sP["Bbr"]], brT.bufs, None, "sp", True)
        m8 = A.mark()
        sel = A.tile([16, 256], BF16)
        ld(sel, sel_in)
        dbuf = [A.tile([32, 128], BF16) for _ in range(3)]
        di = 0
        for gq in range(8):
            for n in range(3):
                dd = dbuf[di % 3]
                di += 1
                ld(dd, gout[gq, n].rearrange("c p f -> p c f"), extra_reads=[Bgout])
                pb = bank(di % 2)
                for q in range(2):
                    for c in range(16):
                        mm(pb[:, q * 256:(q + 1) * 256], dd[:, q * 16 + c, :], sel[:, c, :], c == 0, c == 15)
                cp(brT[:, n * 8 + gq, 1024:1536], pb, eng=("act" if di % 2 else "dve"))
        A.release(m8)
        wgs = [A.tile([KC, 3, 128], BF16) for _ in range(2)]
        wbs = [A.tile([8, 3, 128], BF16) for _ in range(2)]
        for dc in range(KC):
            wg, wbb = wgs[dc % 2], wbs[dc % 2]
            for n in range(3):
                wload(wg[:, :, n, :], w_gate[l, :, n * D + dc * 128: n * D + (dc + 1) * 128].rearrange("(k p) m -> p k m", p=128), part=True)
                wload(wbb[:, :, n, :], w_br[l, n, :, dc * 128:(dc + 1) * 128].rearrange("(k p) m -> p k m", p=128), part=True)
            for bi in range(NBL):
                sl = slice(bi * 512, (bi + 1) * 512)
                gp = [bank(n) for n in range(3)]
                bp = [bank(3 + n) for n in range(3)]
                for n in range(3):
                    for kc in range(KC):
                        mm(gp[n], wg[:, kc, n, :], hts[bi][:, kc, :], kc == 0, kc == KC - 1)
                    for wc in range(8):
                        mm(bp[n], wbb[:, wc, n, :], brT[:, n * 8 + wc, sl], wc == 0, wc == 7)
                m1 = A.mark()
                sg = [A.tile([512], F32) for _ in range(3)]
                for n in range(3):
                    act(sg[n], gp[n], AF.Sigmoid)
                    tt(sg[n], sg[n], bp[n], ALU.mult)
                tt(sg[0], sg[0], sg[1], ALU.add)
                mo = A.tile([512], BF16)
                tt(mo, sg[0], sg[2], ALU.add)
                st(mixd[dc, :, sl], mo, [Bmixd])
                A.release(m1)
        A.release(m0)
        m0 = A.mark()
        mix = A.tile([KC, 1536], BF16)
        dma(mix.ap, mixd.rearrange("k p n -> p k n"), [Bmixd], mix.bufs, None, "sp")
        wos = [A.tile([KC, 128], BF16) for _ in range(2)]
        g.xtpool = [[A.tile([512], F32) for _ in range(NBL)] for _ in range(2)]
        g.xtctr = 0
        nxt = xT_prefetch(0)
        for dc in range(KC):
            wo = wos[dc % 2]
            wload(wo, w_out[l, :, dc * 128:(dc + 1) * 128].rearrange("(k p) m -> p k m", p=128))
            cur = nxt
            if dc + 1 < KC:
                nxt = xT_prefetch(dc + 1)
            pbs = [bank((dc % 2) * 3 + bi) for bi in range(NBL)]
            for bi in range(NBL):
                for kc in range(KC):
                    mm(pbs[bi], wo[:, kc, :], mix[:, kc, bi * 512:(bi + 1) * 512], kc == 0, kc == KC - 1)
            xT_update(dc, pbs, 2, cur)
        A.release(m0)

    def ffn(l):
        g.l = l
        m0 = A.mark()
        hts = [A.tile([KC, 512], BF16) for _ in range(NBL)]
        for bi, (s, b) in enumerate(BL):
            norm_phase(s, l, 1, [b], [hts[bi]], False)
        HC = FC // 2
        for half in range(2):
            m1 = A.mark()
            actT = A.tile([HC, 1536], BF16)
            wgu = [A.tile([KC, 2, 128], BF16) for _ in range(2)]
            for ci in range(HC):
                c = half * HC + ci
                w = wgu[ci % 2]
                wload(w[:, :, 0, :], w_ffg[l, :, c * 128:(c + 1) * 128].rearrange("(k p) m -> p k m", p=128), part=True)
                wload(w[:, :, 1, :], w_ffu[l, :, c * 128:(c + 1) * 128].rearrange("(k p) m -> p k m", p=128), part=True)
                for bi in range(NBL):
                    gp = bank(2 * bi)
                    up = bank(2 * bi + 1)
                    for kc in range(KC):
                        mm(gp, w[:, kc, 0, :], hts[bi][:, kc, :], kc == 0, kc == KC - 1)
                    for kc in range(KC):
                        mm(up, w[:, kc, 1, :], hts[bi][:, kc, :], kc == 0, kc == KC - 1)
                    m2 = A.mark()
                    sg = A.tile([512], F32)
                    act(sg, gp, AF.Silu)
                    tt(actT[:, ci, bi * 512:(bi + 1) * 512], sg, up, ALU.mult)
                    A.release(m2)
            wds = [A.tile([HC, 128], BF16) for _ in range(2)]
            g.xtpool = [[A.tile([512], F32) for _ in range(NBL)] for _ in range(2)]
            g.xtctr = 0
            nxt = xT_prefetch(0)
            for dc in range(KC):
                wd = wds[dc % 2]
                wload(wd, w_ffd[l, half * HC * 128:(half + 1) * HC * 128, dc * 128:(dc + 1) * 128].rearrange("(k p) m -> p k m", p=128))
                cur = nxt
                if dc + 1 < KC:
                    nxt = xT_prefetch(dc + 1)
                pbs = [bank((dc % 2) * 3 + bi) for bi in range(NBL)]
                for bi in range(NBL):
                    for ci in range(HC):
                        mm(pbs[bi], wd[:, ci, :], actT[:, ci, bi * 512:(bi + 1) * 512], ci == 0, ci == HC - 1)
                xT_update(dc, pbs, 5, cur)
            A.release(m1)
        A.release(m0)

    def final_norm(s):
        for b in range(s["NB"]):
            m0 = A.mark()
            yT = A.tile([KC, 512], F32)
            norm_phase(s, 0, 2, [b], [yT], False)
            for q in range(4):
                yo = A.tile([D], F32)
                for k4 in range(4):
                    pb = bank(k4 % 2)
                    for i in range(4):
                        kc = k4 * 4 + i
                        tr(pb[:, i * 128:(i + 1) * 128], yT[:, kc, q * 128:(q + 1) * 128])
                    cp(yo[:, k4 * 512:(k4 + 1) * 512], pb, eng=("act" if k4 % 2 else "dve"))
                st(yout[s["name"]][b * 512 + q * 128: b * 512 + (q + 1) * 128, :], yo, [Bout])
            A.release(m0)

    import os
    KSTOP = int(os.environ.get("KSTOP", "0"))

    class _Stop(Exception):
        pass

    def ck(n):
        if KSTOP == n + 10 * g.lcur:
            raise _Stop()

    def _main_schedule():
        load_xT_layer0(sO)
        load_xT_layer0(sP)
        for l in range(LAYERS):
            g.lcur = l
            layer_setup(l)
            m0 = A.mark()
            ho = A.tile([KC, 512], BF16)
            norm_phase(sO, l, 0, [0], [ho], True)
            A.release(m0)
            ck(1)
            allgather(sO["hT"], sS["hT"], [sO["BhT"][0]], [sS["BhT"][0]], "h%d" % l)
            ck(2)
            m0 = A.mark()
            hts = [A.tile([KC, 512], BF16) for _ in range(sP["NB"])]
            norm_phase(sP, l, 0, list(range(sP["NB"])), hts, True)
            mixers(sP, l, hts)
            A.release(m0)
            hyena2(sP, l)
            ck(3)
            m0 = A.mark()
            mixers(sS, l, None)
            A.release(m0)
            ck(4)
            hyena2(sS, l)
            ck(5)
            allgather(gin, gout, [Bgin], [Bgout], "b%d" % l)
            m0 = A.mark()
            pb = bank(0, 128, 64)
            tr(pb, st_sb)
            so = A.tile([128], F32, parts=64)
            cp(so, pb)
            for sq_ in range(NPS):
                st(nst[sq_, l].rearrange("d (g n) -> (d g) n", n=128), so[sq_ * 16:(sq_ + 1) * 16, :], [Bout])
            A.release(m0)
            ck(6)
            phaseB(l)
            ck(7)
            ffn(l)
            ck(8)
        final_norm(sP)
        final_norm(sO)

    try:
        _main_schedule()
    except _Stop:
        pass
    P.emit(es)
    es.close()
    return nc


_CACHE = {}


def kernel(**inp):
    inp = {k: np.asarray(v) for k, v in inp.items()}
    consts = _consts()
    prm = np.stack([_prm_rows(inp, l)[0] for l in range(LAYERS)])
    idx = _prm_rows(inp, 0)[1]
    RP = prm.shape[1]
    if "nc" not in _CACHE:
        _CACHE["nc"] = build(idx, RP)
    nc = _CACHE["nc"]
    f32 = lambda a: np.ascontiguousarray(a, dtype=np.float32)
    shared = {}
    for k in ("w_in", "w_gate", "att_lambda", "lru_wa", "lru_wx", "hy_w1", "hy_w2", "hy_w3", "hy_b3",
              "hy_d", "w_br", "w_out", "w_ff_gate", "w_ff_up", "w_ff_down"):
        shared[k] = f32(inp[k])
    shared["hy_decay"] = f32(inp["hy_decay"].reshape(LAYERS, 4096))
    shared["prm"] = prm
    shared["cvecs"] = f32(np.stack([inp["c_ctx"], inp["c"][0], inp["c"][1]]))
    for k, vv in consts.items():
        shared[k] = vv
    in_maps = []
    w3r = inp["hy_w3"].reshape(LAYERS, 64, 4, 1024)
    b3r = inp["hy_b3"].reshape(LAYERS, 4, 1024)
    dcr = inp["hy_decay"].reshape(LAYERS, 4, 1024)
    for c in range(NCORES):
        m = dict(shared)
        ch = slice(c * 128, (c + 1) * 128)
        m["w_mod_s"] = f32(inp["w_mod"][:, :, c * 1536:(c + 1) * 1536])
        m["b_mod_s"] = f32(inp["b_mod"][:, c * 1536:(c + 1) * 1536])
        m["xp"] = f32(inp["x_prompt"][c * NPS:(c + 1) * NPS].reshape(NPS * LP, D))
        m["xo"] = f32(inp["x_sample"][:, c * 256:(c + 1) * 256, :].reshape(TO, D))
        m["ck"] = f32(inp["cache_k"][:, :, :, c, :])
        m["cv"] = f32(inp["cache_v"][:, :, :, c, :])
        m["s0"] = f32(inp["state_lru"][:, :, :, ch])
        cols = np.concatenate([np.arange(off + c * 128, off + (c + 1) * 128) for off in
                               (0, 1024, 2048, 3072, 4096, 5120, 6144, 7168)])
        m["w_in_s"] = f32(inp["w_in"][:, :, cols])
        m["lru_wa_s"] = f32(inp["lru_wa"][:, :, 2 * c:2 * c + 2])
        m["lru_wx_s"] = f32(inp["lru_wx"][:, :, 2 * c:2 * c + 2])
        m["hy_w3_s"] = f32(w3r[:, :, :, ch].reshape(LAYERS, 64, 512))
        m["hy_b3_s"] = f32(b3r[:, :, ch].reshape(LAYERS, 512))
        m["hy_decay_s"] = f32(dcr[:, :, ch].reshape(LAYERS, 512))
        m["hy_d_s"] = f32(inp["hy_d"][:, :, ch])
        rolled = _roll_inputs(inp, c)
        m["prm_s"] = np.stack([_prm_rows(rolled, l)[0] for l in range(LAYERS)])
        sel = np.zeros((128, 16, 256), np.float32)
        for j in range(256):
            t = 256 * c + j
            sel[t % 128, t // 128, j] = 1.0
        m["sel"] = _bf(sel)
        in_maps.append(m)
    res = run_bass_kernel_spmd(nc, in_maps, core_ids=list(range(NCORES)))
    R = res.results
    y_prompt = np.concatenate([R[c]["yp"].reshape(NPS, LP, D) for c in range(NCORES)], 0)
    y_sample = np.concatenate([R[c]["yo"].reshape(2, 256, D) for c in range(NCORES)], 1)
    new_ck = np.concatenate([R[c]["nck"] for c in range(NCORES)], 0)
    new_cv = np.concatenate([R[c]["ncv"] for c in range(NCORES)], 0)
    new_st = np.concatenate([R[c]["nst"] for c in range(NCORES)], 0)
    return (y_prompt.astype(np.float32), y_sample.astype(np.float32), new_ck.astype(np.float32),
            new_cv.astype(np.float32), new_st.astype(np.float32))
```

```python
import math
from contextlib import ExitStack
import numpy as np
import ml_dtypes
import concourse.bass as bass
import concourse.mybir as mybir
from concourse.bass_utils import run_bass_kernel_spmd

F32 = mybir.dt.float32
BF16 = mybir.dt.bfloat16
AF = mybir.ActivationFunctionType
ALU = mybir.AluOpType

D = 2048
KC = 16
NH = 8
DFF = 5632
FC = 44
LAYERS = 2
NPS = 4
LP = 256
LS = 2048
PAST = 256
NCORES = 8
LAM_INIT = [0.8 - 0.6 * math.exp(-0.3 * l) for l in range(LAYERS)]


class Buf:
    __slots__ = ("name", "last_cw", "wr_dma", "rd_eng", "rd_dma", "sem", "dma_count", "is_holder")

    def __init__(self, name):
        self.name = name
        self.last_cw = None
        self.wr_dma = set()
        self.rd_eng = {}
        self.rd_dma = set()
        self.sem = None
        self.dma_count = 0
        self.is_holder = False


class Op:
    __slots__ = ("eng", "fn", "is_dma", "cc", "holder", "marked", "val", "w_eng", "w_buf", "_idx")


class V:
    __slots__ = ("ap", "bufs")

    def __init__(self, ap, bufs):
        self.ap = ap
        self.bufs = bufs

    def __getitem__(self, idx):
        return V(self.ap[idx], self.bufs)

    def rr(self, pat, **kw):
        return V(self.ap.rearrange(pat, **kw), self.bufs)

    def bc(self, shape):
        return V(self.ap.to_broadcast(shape), self.bufs)


class Prog:
    ENGS = ("pe", "act", "dve", "pool", "sp")

    def __init__(self, nc):
        self.nc = nc
        self.ops = {e: [] for e in self.ENGS}
        self.holders = []

    def op(self, eng, fn, reads=(), writes=(), dma=False, holder=None, part=False, cc=False):
        o = Op()
        o.eng = eng
        o.fn = fn
        o.is_dma = dma
        o.cc = cc
        o.holder = None
        o.marked = False
        o.val = 0
        o._idx = len(self.ops[eng])
        w_eng = {}
        w_buf = {}

        def add_dep(d):
            if d is None:
                return
            if d.eng == eng and eng == "pe":
                return
            cur = w_eng.get(d.eng)
            if cur is None or cur._idx < d._idx:
                w_eng[d.eng] = d

        for b in reads:
            add_dep(b.last_cw)
            for h in b.wr_dma:
                w_buf[h] = h.dma_count
        for b in writes:
            add_dep(b.last_cw)
            if not dma:
                for h in b.wr_dma:
                    w_buf[h] = h.dma_count
            for d in b.rd_eng.values():
                add_dep(d)
            for h in b.rd_dma:
                w_buf[h] = h.dma_count
        for d in w_eng.values():
            d.marked = True
        o.w_eng = w_eng
        o.w_buf = w_buf
        for b in reads:
            if dma:
                b.rd_dma.add(holder)
            else:
                b.rd_eng[eng] = o
        for b in writes:
            b.rd_eng = {}
            b.rd_dma = set()
            if dma:
                if part:
                    b.wr_dma.add(holder)
                else:
                    b.wr_dma = {holder}
            else:
                b.last_cw = o
                b.wr_dma = set()
        if dma:
            if not holder.is_holder:
                holder.is_holder = True
                self.holders.append(holder)
            holder.dma_count += (1 if cc else 16)
            o.holder = holder
        self.ops[eng].append(o)
        return o

    def emit(self, es):
        nc = self.nc
        esem = {e: es.enter_context(nc.semaphore("es_" + e)) for e in self.ENGS}
        for i, h in enumerate(self.holders):
            h.sem = es.enter_context(nc.semaphore("hs%d" % i))
        for e in self.ENGS:
            c = 0
            for o in self.ops[e]:
                if o.marked:
                    c += 1
                    o.val = c
        block = es.enter_context(nc.Block())
        prog = self

        def run(e, engobj):
            waited = {}
            for o in prog.ops[e]:
                for de, d in o.w_eng.items():
                    s = esem[de]
                    if waited.get(de, 0) < d.val:
                        engobj.wait_ge(s, d.val)
                        waited[de] = d.val
                for h, v in o.w_buf.items():
                    if waited.get(h, 0) < v:
                        engobj.wait_ge(h.sem, v)
                        waited[h] = v
                ins = o.fn(engobj)
                if o.is_dma and o.cc:
                    ins.then_inc(o.holder.sem)
                elif o.is_dma:
                    ins.then_inc(o.holder.sem, 16)
                elif o.marked:
                    ins.then_inc(esem[e], 1)
            if e == "sp":
                for h in prog.holders:
                    if waited.get(h, 0) < h.dma_count:
                        engobj.wait_ge(h.sem, h.dma_count)
                for e2 in prog.ENGS:
                    if e2 == e:
                        continue
                    last = None
                    for o in prog.ops[e2]:
                        if o.marked:
                            last = o
                    if last is not None and waited.get(e2, 0) < last.val:
                        engobj.wait_ge(esem[e2], last.val)

        @block.tensor
        def _(t):
            run("pe", t)

        @block.scalar
        def _(t):
            run("act", t)

        @block.vector
        def _(t):
            run("dve", t)

        @block.gpsimd
        def _(t):
            run("pool", t)

        @block.sync
        def _(t):
            run("sp", t)


PAGE = 1024


class Arena:
    def __init__(self, nc, es, nbytes):
        self.n = nbytes
        self.t = es.enter_context(nc.sbuf_tensor("arena", [128, nbytes // 2], BF16))
        self.pages = [Buf("pg%d" % i) for i in range(nbytes // PAGE)]
        self.ptr = 0

    def mark(self):
        return self.ptr

    def release(self, m):
        self.ptr = m

    def tile(self, free, dt, parts=128):
        sz = 4 if dt == F32 else 2
        n = int(np.prod(free)) * sz
        npg = (n + PAGE - 1) // PAGE
        off = self.ptr
        self.ptr += npg * PAGE
        assert self.ptr <= self.n, "arena overflow %d" % self.ptr
        ap = self.t[:, off // 2: off // 2 + n // 2]
        if dt == F32:
            ap = ap.bitcast(F32)
        if len(free) == 2:
            ap = ap.rearrange("p (a b) -> p a b", b=free[1])
        elif len(free) == 3:
            ap = ap.rearrange("p (a b c) -> p a b c", b=free[1], c=free[2])
        if parts < 128:
            ap = ap[0:parts]
        return V(ap, self.pages[off // PAGE: off // PAGE + npg])


def _bf(x):
    return np.asarray(x, np.float32).astype(ml_dtypes.bfloat16)


def _dft_tables(L):
    k = np.arange(L, dtype=np.int64)
    t = np.arange(L, dtype=np.int64)
    m = (np.outer(t, k) % (2 * L)).astype(np.float64)
    ang = np.pi * m / L
    fC = np.cos(ang)
    fS = -np.sin(ang)
    fS[:, 0] = np.where(t % 2 == 0, 1.0, -1.0)
    iC = np.cos(ang.T) / L
    iS = -np.sin(ang.T) / L
    iC[0, :] = 1.0 / (2 * L)
    iS[0, :] = np.where(t % 2 == 0, 1.0, -1.0) / (2 * L)
    n = L // 128

    def arr(M):
        return np.ascontiguousarray(M.reshape(n, 128, n, 128).transpose(2, 1, 0, 3))

    return _bf(arr(fC)), _bf(arr(fS)), _bf(arr(iC)), _bf(arr(iS))


def _consts():
    c = {}
    c["ident"] = np.eye(128, dtype=np.float32)
    perm = np.zeros((128, 128), np.float32)
    for m in range(128):
        s, r = divmod(m, 64)
        a, r2 = divmod(r, 32)
        p, f = divmod(r2, 16)
        k = s * 64 + a * 32 + (1 - p) * 16 + f
        perm[k, m] = 1.0
    c["perm"] = perm
    inv = 10000.0 ** (-np.arange(16, dtype=np.float32) / 16)
    tt = np.arange(LS)
    pos = np.stack([tt // 64, tt % 64], -1).astype(np.float32)
    ang = pos[:, :, None] * inv
    cs, sn = np.cos(ang), np.sin(ang)
    cosT = np.zeros((128, LS), np.float32)
    sinT = np.zeros((128, LS), np.float32)
    for m in range(128):
        s, r = divmod(m, 64)
        a, r2 = divmod(r, 32)
        p, f = divmod(r2, 16)
        cosT[m] = cs[:, a, f]
        sinT[m] = (-sn[:, a, f]) if p == 0 else sn[:, a, f]
    c["ropec"] = cosT
    c["ropes"] = sinT
    for L, nm in ((LP, "p"), (LS, "s")):
        fC, fS, iC, iS = _dft_tables(L)
        c["fC" + nm], c["fS" + nm], c["iC" + nm], c["iS" + nm] = fC, fS, iC, iS
        t = np.linspace(0.0, 1.0, L, dtype=np.float32)[:, None]
        w = (2.0 * np.float32(math.pi) * np.arange(L, dtype=np.float32)[:, None] / np.float32(L)).astype(np.float32)
        f = np.linspace(1e-4, 7, 8, dtype=np.float32)[None]
        z = np.concatenate([t, np.cos(f * w), -np.sin(f * w)], -1).astype(np.float32)
        c["zT" + nm] = np.ascontiguousarray(z.T)
        c["negt" + nm] = np.ascontiguousarray((-t[:, 0]).reshape(L // 128, 128).T)
    return c


def _prm_rows(inp, l):
    rows = []
    idx = {}

    def add(name, arr):
        a = np.asarray(arr, np.float32).reshape(-1, 128)
        idx[name] = len(rows)
        rows.extend(list(a))

    def add64(name, arr):
        a = np.zeros(128, np.float32)
        a[:64] = arr
        idx[name] = len(rows)
        rows.append(a)

    add("norm_mix", inp["norm_mix"][l])
    add("norm_ffn", inp["norm_ffn"][l])
    add("final_norm", inp["final_norm"])
    add("subln", inp["att_subln"][l])
    add("lru_cw", inp["lru_conv_w"][l])
    add("lru_cb", inp["lru_conv_b"][l])
    add("lru_ba", inp["lru_ba"][l])
    add("lru_bx", inp["lru_bx"][l])
    add("lru_lam", inp["lru_lambda"][l])
    add("hy_cw", inp["hy_conv_w"][l])
    add("hy_cb", inp["hy_conv_b"][l])
    add64("hy_freq", inp["hy_freq"][l])
    add64("hy_b1", inp["hy_b1"][l])
    add64("hy_b2", inp["hy_b2"][l])
    R = len(rows)
    RP = ((R + 127) // 128) * 128
    out = np.zeros((RP, 128), np.float32)
    out[:R] = np.stack(rows)
    return out, idx


def _roll_inputs(inp, c):
    o = dict(inp)
    sh = -c * 128
    for k in ("lru_conv_w", "lru_conv_b", "lru_ba", "lru_bx", "lru_lambda"):
        o[k] = np.roll(inp[k], sh, axis=-1)
    o["hy_conv_w"] = np.roll(inp["hy_conv_w"].reshape(LAYERS, 3, 3, 1024), sh, axis=-1).reshape(LAYERS, 3, 3072)
    o["hy_conv_b"] = np.roll(inp["hy_conv_b"].reshape(LAYERS, 3, 1024), sh, axis=-1).reshape(LAYERS, 3072)
    return o


class K:
    pass


NV = 3
TO = 512


def build(prm_idx, RP):
    nc = bass.Bass("TRN2", target_bir_lowering=False)
    P = Prog(nc)
    es = ExitStack()
    g = K()

    def din(name, shape, dt=F32):
        return nc.dram_tensor(name, list(shape), dt, kind="ExternalInput").ap()

    def dout(name, shape, dt=F32):
        return nc.dram_tensor(name, list(shape), dt, kind="ExternalOutput").ap()

    def dscr(name, shape, dt):
        return nc.dram_tensor(name, list(shape), dt).ap()

    xin = {"P": din("xp", [NPS * LP, D]), "O": din("xo", [TO, D])}
    cv_in = din("cvecs", [NV, D])
    ck_in = din("ck", [2, LAYERS, PAST, 128])
    cvv_in = din("cv", [2, LAYERS, PAST, 128])
    s0_in = din("s0", [2, LAYERS, 2, 128])
    w_mod = din("w_mod_s", [LAYERS, D, 1536])
    b_mod = din("b_mod_s", [LAYERS, 1536])
    w_in = din("w_in", [LAYERS, D, 8192])
    w_in_s = din("w_in_s", [LAYERS, D, 1024])
    w_gate = din("w_gate", [LAYERS, D, 3 * D])
    att_lambda = din("att_lambda", [LAYERS, 4, 64])
    lru_wa = din("lru_wa", [LAYERS, 2, 16, 64, 64])
    lru_wx = din("lru_wx", [LAYERS, 2, 16, 64, 64])
    lru_wa_s = din("lru_wa_s", [LAYERS, 2, 2, 64, 64])
    lru_wx_s = din("lru_wx_s", [LAYERS, 2, 2, 64, 64])
    hy_w1 = din("hy_w1", [LAYERS, 17, 64])
    hy_w2 = din("hy_w2", [LAYERS, 64, 64])
    hyw = {"P": dict(w3=din("hy_w3", [LAYERS, 64, 4096]), b3=din("hy_b3", [LAYERS, 4096]),
                     dec=din("hy_decay", [LAYERS, 4096]), d=din("hy_d", [LAYERS, 2, 1024]), cs=1024),
           "S": dict(w3=din("hy_w3_s", [LAYERS, 64, 512]), b3=din("hy_b3_s", [LAYERS, 512]),
                     dec=din("hy_decay_s", [LAYERS, 512]), d=din("hy_d_s", [LAYERS, 2, 128]), cs=128)}
    w_br = din("w_br", [LAYERS, 3, 1024, D])
    w_out = din("w_out", [LAYERS, D, D])
    w_ffg = din("w_ff_gate", [LAYERS, D, DFF])
    w_ffu = din("w_ff_up", [LAYERS, D, DFF])
    w_ffd = din("w_ff_down", [LAYERS, DFF, D])
    prm_in = {"P": din("prm", [LAYERS, RP, 128]), "S": din("prm_s", [LAYERS, RP, 128])}
    ident_in = din("ident", [128, 128])
    perm_in = din("perm", [128, 128])
    ropec_in = din("ropec", [128, LS])
    ropes_in = din("ropes", [128, LS])
    sel_in = din("sel", [128, 16, 256], BF16)
    tabs = {}
    for nm, L in (("p", LP), ("s", LS)):
        n = L // 128
        for t4 in ("fC", "fS", "iC", "iS"):
            tabs[t4 + nm] = din(t4 + nm, [n, 128, n, 128], BF16)
        tabs["zT" + nm] = din("zT" + nm, [17, L])
        tabs["negt" + nm] = din("negt" + nm, [128, n])
    yout = {"P": dout("yp", [NPS * LP, D]), "O": dout("yo", [TO, D])}
    nck = dout("nck", [NPS, LAYERS, LP, NH, 128])
    ncv = dout("ncv", [NPS, LAYERS, LP, NH, 128])
    nst = dout("nst", [NPS, LAYERS, 2, 1024])
    Bout = Buf("outs")

    sets = {
        "P": dict(name="P", nseq=NPS, L=LP, T=NPS * LP, rope=False, ctx=False, tb="p", NB=2,
                  vl=[(0, 512, 0)]),
        "S": dict(name="S", nseq=2, L=LS, T=2 * LS, rope=True, ctx=True, tb="s", NB=8),
        "O": dict(name="O", T=TO, NB=1, vl=[(0, 256, 1), (256, 512, 2)]),
    }
    for nm in ("P", "O"):
        s = sets[nm]
        T = s["T"]
        s["xT"] = dscr("xT" + nm, [D, T], F32)
        s["BxT"] = [Buf("xT%s%d" % (nm, i)) for i in range(T // 512)]
        s["hT"] = dscr("hT" + nm, [T // 512, 128, KC, 512], BF16)
        s["BhT"] = [Buf("hT%s%d" % (nm, i)) for i in range(T // 512)]
    sP, sS, sO = sets["P"], sets["S"], sets["O"]
    sP["br"] = dscr("brP", [24, 128, sP["T"]], BF16)
    sP["Bbr"] = Buf("brP")
    sS["hT"] = dscr("hTS", [8, 128, KC, 512], BF16)
    sS["BhT"] = [Buf("hTSall")] * 8
    gin = dscr("gin", [3, 32, 128, 128], BF16)
    Bgin = Buf("gin")
    gout = dscr("gout", [8, 3, 32, 128, 128], BF16)
    Bgout = Buf("gout")
    for nm in ("P", "S"):
        s = sets[nm]
        s["hyt"] = dscr("hyt" + nm, [3, s["nseq"], s["L"] // 128, 128, 1024 if nm == "P" else 128], F32)
        s["Bhyt"] = Buf("hyt" + nm)
    mixd = dscr("mixd", [KC, 128, 1536], BF16)
    Bmixd = Buf("mixd")

    def sb(name, shape, dt):
        t = es.enter_context(nc.sbuf_tensor(name, list(shape), dt))
        return V(t[:], [Buf(name)])

    ident = sb("ident_sb", [128, 128], F32)
    perm = sb("perm_sb", [128, 128], F32)
    ones_bf = sb("ones_bf", [128, 128], BF16)
    onesr = sb("onesr", [1, NV], F32)
    prm = {"P": [sb("prm%d" % l, [128, RP], F32) for l in range(LAYERS)],
           "S": [sb("prms%d" % l, [128, RP], F32) for l in range(LAYERS)]}
    modv = sb("modv", [128, LAYERS, 96, NV], F32)
    modA = sb("modA", [128, LAYERS, 2, NV, KC], F32)
    scal = sb("scal", [128, 64], F32)
    lruw = {"P": sb("lruw", [128, 32, 128], BF16), "S": sb("lruws", [128, 4, 128], BF16)}
    lruc = {"P": sb("lruc", [128, 16, 2], F32), "S": sb("lrucs", [128, 16, 2], F32)}
    lam_sb = sb("lam_sb", [128, 256], F32)
    hyp = sb("hyp", [128, 8], F32)
    hyw1 = sb("hyw1", [17, 64], F32)
    hyw2 = sb("hyw2", [64, 64], F32)
    hdn = {"p": sb("hdn_p", [65, LP], F32), "s": sb("hdn_s", [65, LS], F32)}
    st_sb = sb("st_sb", [128, 64], F32)
    csil = sb("csil", [128, KC, NV], F32)

    psum_t = es.enter_context(nc.psum_tensor("psum", [128, 4096], F32))
    Bps = [Buf("ps%d" % i) for i in range(8)]

    def bank(i, n=512, parts=128):
        return V(psum_t[0:parts, i * 512: i * 512 + n], [Bps[i]])

    A = Arena(nc, es, 168 * 1024)

    def rb(*vs):
        out = []
        for v in vs:
            if isinstance(v, V):
                out.extend(v.bufs)
        return out

    def ap_(v):
        return v.ap if isinstance(v, V) else v

    def mm(out, lhsT, rhs, start, stop):
        P.op("pe", lambda e: e.matmul(out.ap, lhsT=lhsT.ap, rhs=rhs.ap, start=start, stop=stop),
             reads=rb(lhsT, rhs), writes=out.bufs)

    def tr(out, in_):
        P.op("pe", lambda e: e.transpose(out.ap, in_.ap, ident.ap[0:in_.ap.shape[0], 0:in_.ap.shape[0]]),
             reads=rb(in_, ident), writes=out.bufs)

    def act(out, in_, func, scale=1.0, bias=0.0, eng="act"):
        P.op(eng, lambda e: e.activation(out=out.ap, in_=in_.ap, func=func, scale=ap_(scale), bias=ap_(bias)),
             reads=rb(in_, scale, bias), writes=out.bufs)

    def tt(out, in0, in1, op, eng="dve"):
        P.op(eng, lambda e: e.tensor_tensor(out=out.ap, in0=in0.ap, in1=in1.ap, op=op),
             reads=rb(in0, in1), writes=out.bufs)

    def ts(out, in0, s1, s2, op0, op1=None, eng="dve"):
        if op1 is None:
            P.op(eng, lambda e: e.tensor_scalar(out=out.ap, in0=in0.ap, scalar1=ap_(s1), scalar2=None, op0=op0),
                 reads=rb(in0, s1), writes=out.bufs)
        else:
            P.op(eng, lambda e: e.tensor_scalar(out=out.ap, in0=in0.ap, scalar1=ap_(s1), scalar2=ap_(s2), op0=op0, op1=op1),
                 reads=rb(in0, s1, s2), writes=out.bufs)

    def stt(out, in0, scalar, in1, op0, op1):
        P.op("dve", lambda e: e.scalar_tensor_tensor(out=out.ap, in0=in0.ap, scalar=ap_(scalar), in1=in1.ap, op0=op0, op1=op1),
             reads=rb(in0, scalar, in1), writes=out.bufs)

    def cp(out, in_, eng="dve"):
        if eng == "act":
            act(out, in_, AF.Copy)
        else:
            P.op(eng, lambda e: e.tensor_copy(out=out.ap, in_=in_.ap), reads=rb(in_), writes=out.bufs)

    def memset(out, val, eng="dve"):
        P.op(eng, lambda e: e.memset(out.ap, val), writes=out.bufs)

    def scan(out, d0, d1, init):
        P.op("dve", lambda e: e.tensor_tensor_scan(out=out.ap, data0=d0.ap, data1=d1.ap, initial=ap_(init),
                                                   op0=ALU.mult, op1=ALU.add),
             reads=rb(d0, d1, init), writes=out.bufs)

    def recip(out, in_):
        P.op("dve", lambda e: e.reciprocal(out=out.ap, in_=in_.ap), reads=rb(in_), writes=out.bufs)

    NHOLD = 16
    hold = {"sp": [Buf("hsp%d" % i) for i in range(NHOLD)], "pool": [Buf("hpl%d" % i) for i in range(NHOLD)]}
    hctr = {"sp": 0, "pool": 0}

    def dma(out_ap, in_ap, reads, writes, holder=None, q="sp", part=False):
        h = hold[q][hctr[q] % NHOLD]
        hctr[q] += 1
        P.op(q, lambda e: e.dma_start(out=out_ap, in_=in_ap), reads=reads, writes=writes, dma=True, holder=h, part=part)

    def ld(dst, src_ap, q="sp", extra_reads=(), part=False):
        dma(dst.ap, src_ap, list(extra_reads), dst.bufs, None, q, part)

    def st(dst_ap, src, dbufs, holder=None):
        dma(dst_ap, src.ap, src.bufs, dbufs, None, "sp", True)

    def wload(dst, src_ap, part=False):
        ld(dst, src_ap, q="pool", part=part)

    def allgather(src_ap, dst_ap, rbufs, wbufs, name):
        h = Buf("cc_" + name)
        P.op("pool", lambda e: e.collective_compute("AllGather", ALU.bypass, replica_groups=[list(range(NCORES))],
                                                    ins=[src_ap.opt()], outs=[dst_ap.opt()]),
             reads=rbufs, writes=wbufs, dma=True, holder=h, cc=True)

    ld(ident, ident_in)
    ld(perm, perm_in)
    memset(ones_bf, 1.0)
    memset(onesr, 1.0)
    memset(st_sb, 0.0)
    for kind in ("P", "S"):
        for l in range(LAYERS):
            for c0 in range(RP // 128):
                m0 = A.mark()
                t0 = A.tile([128], F32)
                ld(t0, prm_in[kind][l, c0 * 128:(c0 + 1) * 128, :])
                pb = bank(c0 % 2, 128)
                tr(pb, t0)
                cp(prm[kind][l][:, c0 * 128:(c0 + 1) * 128], pb)
                A.release(m0)

    def pcol(l, name, r=0, n=1, kind="P"):
        c = prm_idx[name] + r
        return prm[kind][l][:, c:c + n]

    m0 = A.mark()
    ctok = A.tile([D], F32, parts=NV)
    ld(ctok, cv_in)
    for kc in range(KC):
        pb = bank(kc % 2, NV)
        tr(pb, ctok[:, kc * 128:(kc + 1) * 128])
        act(csil[:, kc, :], pb, AF.Silu)
    A.release(m0)
    mod_in = dscr("mod_in", [128, LAYERS * 12 * NV], F32)
    mod_out = dscr("mod_out", [NCORES, 128, LAYERS * 12 * NV], F32)
    Bmod_in, Bmod_out = Buf("mod_in"), Buf("mod_out")
    m0 = A.mark()
    mstage = A.tile([LAYERS, 12, NV], F32)
    for l in range(LAYERS):
        bmod_sb = A.tile([1536], F32, parts=1)
        ld(bmod_sb, b_mod[l:l + 1, :])
        wb = [A.tile([KC, 512], F32) for _ in range(2)]
        for ct in range(3):
            w = wb[ct % 2]
            ld(w, w_mod[l, :, ct * 512:(ct + 1) * 512].rearrange("(k p) n -> p k n", p=128))
            for mc in range(4):
                col = ct * 4 + mc
                pb = bank(2 + (col % 2), NV)
                for kc in range(KC):
                    mm(pb, w[:, kc, mc * 128:(mc + 1) * 128], csil[:, kc, :], kc == 0, False)
                mm(pb, bmod_sb[:, col * 128:(col + 1) * 128], onesr, False, True)
                cp(mstage[:, l, col, :], pb)
    st(mod_in, mstage.rr("p a b c -> p (a b c)"), [Bmod_in])
    A.release(m0)
    allgather(mod_in, mod_out, [Bmod_in], [Bmod_out], "mod")
    for r in range(NCORES):
        dma(modv.ap[:, :, r * 12:(r + 1) * 12, :], mod_out[r].rearrange("p (a b c) -> p a b c", a=LAYERS, b=12),
            [Bmod_out], modv.bufs, None, "sp", True)
    for l in range(LAYERS):
        for wi, (gn, j) in enumerate((("norm_mix", 1), ("norm_ffn", 4))):
            for v in range(NV):
                ts(modA[:, l, wi, v, :], modv[:, l, j * 16:(j + 1) * 16, v], 1.0, None, ALU.add)
                tt(modA[:, l, wi, v, :], modA[:, l, wi, v, :], pcol(l, gn, 0, KC), ALU.mult)

    def modcol(l, j, kc, v):
        return modv[:, l, j * 16 + kc, v:v + 1]

    eps_n = scal[:, 0:1]
    eps_s = scal[:, 1:2]
    memset(scal[:, 0:1], 1e-6)
    memset(scal[:, 1:2], 1e-5)

    def norm_block(xb, l, wi, vl, dst):
        m0 = A.mark()
        sq = A.tile([KC, 512], BF16)
        act(sq, xb, AF.Square)
        pb = bank(7)
        for kc in range(KC):
            mm(pb, ones_bf, sq[:, kc, :], kc == 0, kc == KC - 1)
        rstd = A.tile([512], F32)
        act(rstd, pb, AF.Ln, scale=1.0 / D, bias=eps_n)
        act(rstd, rstd, AF.Exp, scale=-0.5)
        tmp = A.tile([KC, 512], F32)
        tt(tmp, xb, rstd.rr("p (o n) -> p o n", o=1).bc([128, KC, 512]), ALU.mult)
        for kc in range(KC):
            if wi == 2:
                ts(dst[:, kc, :], tmp[:, kc, :], pcol(0, "final_norm", kc), None, ALU.mult)
            else:
                for (c0, c1, v) in vl:
                    act(dst[:, kc, c0:c1], tmp[:, kc, c0:c1], AF.Identity, scale=modA[:, l, wi, v, kc:kc + 1],
                        bias=modcol(l, 0 if wi == 0 else 3, kc, v))
        A.release(m0)

    def load_xT_layer0(s):
        T = s["T"]
        for b in range(T // 512):
            m0 = A.mark()
            stg = A.tile([KC, 512], F32)
            for q in range(4):
                xt = A.tile([D], F32)
                ld(xt, xin[s["name"]][b * 512 + q * 128: b * 512 + (q + 1) * 128, :])
                for k4 in range(4):
                    pb = bank(k4 % 2)
                    for i in range(4):
                        kc = k4 * 4 + i
                        tr(pb[:, i * 128:(i + 1) * 128], xt[:, kc * 128:(kc + 1) * 128])
                    cp(stg[:, k4 * 4:(k4 + 1) * 4, q * 128:(q + 1) * 128],
                       pb.rr("p (i n) -> p i n", i=4), eng=("act" if k4 % 2 else "dve"))
            st(s["xT"][:, b * 512:(b + 1) * 512].rearrange("(k p) n -> p k n", p=128), stg, [s["BxT"][b]])
            A.release(m0)

    def norm_phase(s, l, wi, blocks, dsts, store_h):
        for b, dst in zip(blocks, dsts):
            m0 = A.mark()
            xb = A.tile([KC, 512], F32)
            ld(xb, s["xT"][:, b * 512:(b + 1) * 512].rearrange("(k p) n -> p k n", p=128), extra_reads=[s["BxT"][b]])
            norm_block(xb, l, wi, s["vl"], dst)
            A.release(m0)
            if store_h:
                st(s["hT"][b], dst, [s["BhT"][b]])

    def layer_setup(l):
        ld(lam_sb, att_lambda[l].rearrange("a b -> (a b)").partition_broadcast(128))
        tt(lam_sb[:, 0:64], lam_sb[:, 0:64], lam_sb[:, 64:128], ALU.mult)
        tt(lam_sb[:, 128:192], lam_sb[:, 128:192], lam_sb[:, 192:256], ALU.mult)
        P.op("dve", lambda e: e.reduce_sum(out=scal.ap[:, 2:3], in_=lam_sb.ap[:, 0:64], axis=mybir.AxisListType.X),
             reads=lam_sb.bufs, writes=scal.bufs)
        P.op("dve", lambda e: e.reduce_sum(out=scal.ap[:, 3:4], in_=lam_sb.ap[:, 128:192], axis=mybir.AxisListType.X),
             reads=lam_sb.bufs, writes=scal.bufs)
        act(scal[:, 2:4], scal[:, 2:4], AF.Exp)
        tt(scal[:, 4:5], scal[:, 3:4], scal[:, 2:3], ALU.subtract)
        ts(scal[:, 4:5], scal[:, 4:5], -LAM_INIT[l], None, ALU.add)
        ts(scal[:, 5:6], pcol(l, "subln"), 1.0 - LAM_INIT[l], None, ALU.mult)
        for kind in ("P", "S"):
            lamc = pcol(l, "lru_lam", 0, 16, kind)
            m0 = A.mark()
            t1 = A.tile([16], F32)
            act(t1, lamc, AF.Exp, scale=-1.0)
            act(t1, t1, AF.Ln, bias=1.0)
            ts(lruc[kind][:, :, 0], t1, -8.0, None, ALU.mult)
            ts(lruc[kind][:, :, 1], t1, -16.0, None, ALU.mult)
            A.release(m0)
            memset(lruw[kind], 0.0)
        for dr in range(2):
            for wi, (wsrc, wsrc_s) in enumerate(((lru_wa, lru_wa_s), (lru_wx, lru_wx_s))):
                for hb in range(2):
                    dma(lruw["S"].ap[hb * 64:(hb + 1) * 64, dr * 2 + wi, hb * 64:(hb + 1) * 64], wsrc_s[l, dr, hb],
                        [], lruw["S"].bufs, None, "pool", True)
                for gg in range(8):
                    slot = (dr * 2 + wi) * 8 + gg
                    for hb in range(2):
                        dma(lruw["P"].ap[hb * 64:(hb + 1) * 64, slot, hb * 64:(hb + 1) * 64], wsrc[l, dr, 2 * gg + hb],
                            [], lruw["P"].bufs, None, "pool", True)
        ts(hyp[:, 0:1], pcol(l, "hy_freq"), 1.0 / 3.0, None, ALU.mult)
        tt(hyp[:, 1:2], hyp[:, 0:1], pcol(l, "hy_b1"), ALU.mult)
        tt(hyp[:, 2:3], hyp[:, 0:1], pcol(l, "hy_b2"), ALU.mult)
        ld(hyw1, hy_w1[l])
        ld(hyw2, hy_w2[l])
        for nm, L in (("p", LP), ("s", LS)):
            hd = hdn[nm]
            for b0 in range(0, L, 512):
                n = min(512, L - b0)
                m0 = A.mark()
                zt = A.tile([n], F32, parts=17)
                ld(zt, tabs["zT" + nm][:, b0:b0 + n])
                h1 = A.tile([n], F32, parts=64)
                sq = A.tile([n], F32, parts=64)
                for li in range(2):
                    pb = bank(0, n, 64)
                    if li == 0:
                        mm(pb, hyw1, zt, True, True)
                    else:
                        mm(pb, hyw2, h1, True, True)
                    dst = h1 if li == 0 else hd[0:64, b0:b0 + n]
                    s3 = A.tile([n], F32, parts=64)
                    act(s3, pb, AF.Sin, scale=hyp[0:64, 0:1], bias=hyp[0:64, 1 + li:2 + li])
                    tt(sq, s3, s3, ALU.mult)
                    ts(sq, sq, -4.0, 3.0, ALU.mult, ALU.add)
                    tt(dst, s3, sq, ALU.mult)
                A.release(m0)
            memset(hd[64:65, :], 1.0)

    def project(l, s, col_list, dsts, hts):
        kind = s["name"]
        ws = []
        for cap in col_list:
            w = g.wpool[g.wctr % len(g.wpool)]
            g.wctr += 1
            wload(w, cap.rearrange("(k p) n -> p k n", p=128))
            ws.append(w)
        for b in range(s["NB"]):
            if hts is not None:
                hb = hts[b]
            else:
                hb = g.hstream[g.hctr % 2]
                g.hctr += 1
                ld(hb, s["hT"][b], extra_reads=[s["BhT"][b]])
            for ci, w in enumerate(ws):
                pb = bank((b * len(ws) + ci) % 2)
                for kc in range(KC):
                    mm(pb, w[:, kc, :], hb[:, kc, :], kc == 0, kc == KC - 1)
                if kind == "P":
                    cp(dsts[ci].rr("p a b -> p (a b)")[:, b * 512:(b + 1) * 512], pb, eng="act")
                else:
                    cp(dsts[ci][:, :, b * 256:(b + 1) * 256], pb.rr("p (q t) -> p q t", q=2), eng="act")

    def mixers(s, l, hts, groups=None):
        T, L, nseq, NB = s["T"], s["L"], s["nseq"], s["NB"]
        kind = s["name"]
        isS = kind == "S"
        TC = L // 128
        nkc = TC + (2 if s["ctx"] else 0)
        nq = min(L, 512)
        neglam = scal[:, 4:5]
        gsub = scal[:, 5:6]
        g.wpool = [A.tile([KC, 128], BF16) for _ in range(5)]
        g.wctr = 0
        if isS:
            g.hstream = [A.tile([KC, 512], BF16) for _ in range(2)]
            g.hctr = 0

        def wcol(off, gI, ci):
            if isS:
                return w_in_s[l, :, ci * 128:(ci + 1) * 128]
            return w_in[l, :, off + gI * 128: off + (gI + 1) * 128]

        def emit_branch_S(n, x32, sq_, t0, ncols):
            nch = ncols // 128
            pb = bank(6, ncols)
            for i in range(nch):
                tr(pb[:, i * 128:(i + 1) * 128], x32[:, i * 128:(i + 1) * 128])
            m9 = A.mark()
            stg = A.tile([nch, 128], BF16)
            cp(stg, pb.rr("p (a n) -> p a n", a=nch), eng="act")
            c0 = sq_ * 16 + t0 // 128
            st(gin[n, c0:c0 + nch].rearrange("c p f -> p c f"), stg, [Bgin])
            A.release(m9)

        for gI in (groups if groups is not None else range(1 if isS else 8)):
            m0 = A.mark()
            q32 = A.tile([nseq, L], F32)
            k32 = A.tile([nseq, L], F32)
            v32 = A.tile([nseq, L], F32)
            project(l, s, [wcol(0, gI, 0), wcol(1024, gI, 1), wcol(2048, gI, 2)], [q32, k32, v32], hts)
            qb = A.tile([nseq, L], BF16)
            kb = A.tile([nseq, nkc * 128], BF16)
            vtok = A.tile([nseq, nkc, 128], BF16)
            if s["rope"]:
                m1 = A.mark()
                rc = A.tile([LS], F32)
                rs = A.tile([LS], F32)
                ld(rc, ropec_in)
                ld(rs, ropes_in)
                for src, dstb in ((q32, qb), (k32, kb)):
                    for sq_ in range(nseq):
                        for b in range(L // 512):
                            sl = slice(b * 512, (b + 1) * 512)
                            pb = bank(2 + b % 2)
                            mm(pb, perm, src[:, sq_, sl], True, True)
                            m3 = A.mark()
                            t1 = A.tile([512], F32)
                            t2 = A.tile([512], F32)
                            tt(t1, pb, rs[:, sl], ALU.mult)
                            tt(t2, src[:, sq_, sl], rc[:, sl], ALU.mult)
                            tt(dstb[:, sq_, sl], t1, t2, ALU.add)
                            A.release(m3)
                A.release(m1)
            else:
                cp(qb, q32)
                cp(kb[:, :, 0:L], k32)
            for sq_ in range(nseq):
                for c in range(TC):
                    pb = bank(2 + c % 2)
                    tr(pb[:, 0:128], v32[:, sq_, c * 128:(c + 1) * 128])
                    cp(vtok[:, sq_, c, :], pb[:, 0:128])
                    if not s["ctx"]:
                        tr(pb[:, 128:256], k32[:, sq_, c * 128:(c + 1) * 128])
                        stg = A.tile([2, 128], F32)
                        cp(stg, pb[:, 0:256].rr("p (a n) -> p a n", a=2), eng="act")
                        st(ncv[sq_, l, c * 128:(c + 1) * 128, gI, :], stg[:, 0, :], [Bout])
                        st(nck[sq_, l, c * 128:(c + 1) * 128, gI, :], stg[:, 1, :], [Bout])
            if s["ctx"]:
                for sq_ in range(nseq):
                    ctk = A.tile([2, 128], F32)
                    ctv = A.tile([2, 128], F32)
                    ld(ctk, ck_in[sq_, l].rearrange("(c p) n -> p c n", p=128))
                    ld(ctv, cvv_in[sq_, l].rearrange("(c p) n -> p c n", p=128))
                    for c in range(2):
                        pb = bank(2 + c)
                        tr(pb[:, 0:128], ctk[:, c, :])
                        cp(kb[:, sq_, (TC + c) * 128:(TC + c + 1) * 128], pb[:, 0:128])
                        cp(vtok[:, sq_, TC + c, :], ctv[:, c, :])
            att_out = None if isS else A.tile([nseq, L], BF16)
            pT = [A.tile([nq], BF16) for _ in range(4 if not isS else 3)]
            pi = 0
            for sq_ in range(nseq):
                for q0 in range(0, L, nq):
                    qs = slice(q0, q0 + nq)
                    if isS:
                        ob = [bank(2, nq), bank(3, nq)]
                        db = [bank(4, nq), bank(5, nq)]
                    else:
                        bset = 2 * (sq_ % 2)
                        ob = [bank(2 + bset)[:, 0:nq], bank(3 + bset)[:, 0:nq]]
                        db = [bank(2 + bset)[:, 256:256 + nq], bank(3 + bset)[:, 256:256 + nq]]
                    for i in range(2):
                        hs = slice(64 * i, 64 * i + 64)
                        if isS:
                            for c in range(nkc):
                                sb_ = bank(c % 2, nq)
                                mm(sb_, kb[hs, sq_, c * 128:(c + 1) * 128], qb[hs, sq_, qs], True, True)
                                pt = pT[pi % 3]
                                pi += 1
                                act(pt, sb_, AF.Exp, scale=0.125)
                                mm(ob[i], vtok[:, sq_, c, :], pt, c == 0, c == nkc - 1)
                                mm(db[i], ones_bf, pt, c == 0, c == nkc - 1)
                        else:
                            pts = []
                            for c in range(nkc):
                                sb_ = bank(c % 2, nq)
                                mm(sb_, kb[hs, sq_, c * 128:(c + 1) * 128], qb[hs, sq_, qs], True, True)
                                pt = pT[pi % 4]
                                pi += 1
                                act(pt, sb_, AF.Exp, scale=0.125)
                                pts.append(pt)
                            for c in range(nkc):
                                mm(ob[i], vtok[:, sq_, c, :], pts[c], c == 0, c == nkc - 1)
                            for c in range(nkc):
                                mm(db[i], ones_bf, pts[c], c == 0, c == nkc - 1)
                    m2 = A.mark()
                    r0 = A.tile([nq], F32)
                    r1 = A.tile([nq], F32)
                    a0 = A.tile([nq], F32)
                    a1 = A.tile([nq], F32)
                    recip(r0, db[0])
                    recip(r1, db[1])
                    tt(a0, ob[0], r0, ALU.mult)
                    tt(a1, ob[1], r1, ALU.mult)
                    stt(a0, a1, neglam, a0, ALU.mult, ALU.add)
                    sqv = A.tile([nq], BF16)
                    act(sqv, a0, AF.Square)
                    mb = bank(6, nq)
                    mm(mb, ones_bf, sqv, True, True)
                    act(r0, mb, AF.Ln, scale=1.0 / 128, bias=eps_s)
                    act(r0, r0, AF.Exp, scale=-0.5)
                    if isS:
                        stt(a1, a0, gsub, r0, ALU.mult, ALU.mult)
                        emit_branch_S(0, a1, sq_, q0, nq)
                    else:
                        stt(att_out[:, sq_, qs], a0, gsub, r0, ALU.mult, ALU.mult)
                    A.release(m2)
            if not isS:
                st(s["br"][0 * 8 + gI], att_out.rr("p a b -> p (a b)"), [s["Bbr"]])
            A.release(m0)

            m0 = A.mark()
            xl = A.tile([nseq, L], F32)
            gl = A.tile([nseq, L], F32)
            xlf = xl.rr("p a b -> p (a b)")
            glf = gl.rr("p a b -> p (a b)")
            project(l, s, [wcol(3072, gI, 3), wcol(4096, gI, 4)], [xl, gl], hts)
            xc = A.tile([nseq, L], F32)
            xcf = xc.rr("p a b -> p (a b)")
            cw = lambda k: pcol(l, "lru_cw", k * 8 + gI, 1, kind)
            ts(xc, xl, cw(2), pcol(l, "lru_cb", gI, 1, kind), ALU.mult, ALU.add)
            stt(xc[:, :, 2:L], xl[:, :, 0:L - 2], cw(0), xc[:, :, 2:L], ALU.mult, ALU.add)
            stt(xc[:, :, 1:L], xl[:, :, 0:L - 1], cw(1), xc[:, :, 1:L], ALU.mult, ALU.add)
            stt(xc[:, :, 0:L - 1], xl[:, :, 1:L], cw(3), xc[:, :, 0:L - 1], ALU.mult, ALU.add)
            xcb = A.tile([T], BF16)
            cp(xcb, xcf)
            hsum = xl
            hsumf = xlf
            useq = nseq if nseq * L <= 1024 else 1
            UL = useq * L
            m1 = A.mark()
            lsets = [(A.tile([UL], F32), A.tile([UL], F32), A.tile([UL], F32)) for _ in range(2)]
            ltmps = [(A.tile([512], F32), A.tile([512], F32)) for _ in range(2)]
            ui = 0
            ti = 0
            for dr in range(2):
                wsl = (dr * 2 + 0) if isS else (dr * 2 + 0) * 8 + gI
                wsl2 = (dr * 2 + 1) if isS else (dr * 2 + 1) * 8 + gI
                for u0 in range(0, nseq, useq):
                    av, bv, hv = lsets[ui % 2]
                    ui += 1
                    for t0 in range(0, UL, 512):
                        sl = slice(t0, t0 + 512)
                        fsl = slice(u0 * L + t0, u0 * L + t0 + 512)
                        r, ig = ltmps[ti % 2]
                        pr = bank(2 + 2 * (ti % 2))
                        pi_ = bank(3 + 2 * (ti % 2))
                        ti += 1
                        mm(pr, lruw[kind][:, wsl, :], xcb[:, fsl], True, True)
                        mm(pi_, lruw[kind][:, wsl2, :], xcb[:, fsl], True, True)
                        act(r, pr, AF.Sigmoid, bias=pcol(l, "lru_ba", dr * 8 + gI, 1, kind))
                        act(ig, pi_, AF.Sigmoid, bias=pcol(l, "lru_bx", dr * 8 + gI, 1, kind))
                        act(av[:, sl], r, AF.Exp, scale=lruc[kind][:, dr * 8 + gI, 0:1])
                        act(r, r, AF.Exp, scale=lruc[kind][:, dr * 8 + gI, 1:2])
                        ts(r, r, -1.0, 1.0, ALU.mult, ALU.add)
                        ts(r, r, 0.0, None, ALU.max)
                        act(r, r, AF.Sqrt)
                        tt(ig, ig, xcf[:, fsl], ALU.mult)
                        tt(bv[:, sl], ig, r, ALU.mult)
                    for sq in range(useq):
                        sq_ = u0 + sq
                        lsl = slice(sq * L, (sq + 1) * L)
                        m2 = A.mark()
                        if s["ctx"]:
                            h0 = A.tile([1], F32)
                            h0r = A.tile([128], F32, parts=1)
                            ld(h0r, s0_in[sq_, l, dr:dr + 1, :])
                            pb = bank(6, 1)
                            tr(pb, h0r)
                            cp(h0, pb)
                            init = h0
                        else:
                            init = 0.0
                        if dr == 0:
                            scan(hv[:, lsl], av[:, lsl], bv[:, lsl], init)
                            if not s["ctx"]:
                                cp(st_sb[:, (sq_ * 2 + 0) * 8 + gI:(sq_ * 2 + 0) * 8 + gI + 1], hv[:, (sq + 1) * L - 1:(sq + 1) * L], eng="act")
                            cp(hsum[:, sq_, :], hv[:, lsl], eng="act")
                        else:
                            scan(hv[:, lsl][:, ::-1], av[:, lsl][:, ::-1], bv[:, lsl][:, ::-1], init)
                            if not s["ctx"]:
                                cp(st_sb[:, (sq_ * 2 + 1) * 8 + gI:(sq_ * 2 + 1) * 8 + gI + 1], hv[:, sq * L:sq * L + 1], eng="act")
                            tt(hsum[:, sq_, :], hsum[:, sq_, :], hv[:, lsl], ALU.add)
                        A.release(m2)
            A.release(m1)
            gg = A.tile([T], F32)
            act(gg, glf, AF.Gelu_apprx_tanh)
            if isS:
                tt(gg, hsumf, gg, ALU.mult)
                for sq_ in range(nseq):
                    for t0 in range(0, L, 512):
                        emit_branch_S(1, gg[:, sq_ * L + t0: sq_ * L + t0 + 512], sq_, t0, 512)
            else:
                lo = A.tile([T], BF16)
                tt(lo, hsumf, gg, ALU.mult)
                st(s["br"][1 * 8 + gI], lo, [s["Bbr"]])
            A.release(m0)

            for wh in range(3):
                m0 = A.mark()
                hx = A.tile([nseq, L], F32)
                project(l, s, [wcol(5120 + wh * 1024, gI, 5 + wh)], [hx], hts)
                hc = A.tile([nseq, L], F32)
                hw = lambda k: pcol(l, "hy_cw", k * 24 + wh * 8 + gI, 1, kind)
                ts(hc, hx, hw(1), pcol(l, "hy_cb", wh * 8 + gI, 1, kind), ALU.mult, ALU.add)
                stt(hc[:, :, 1:L], hx[:, :, 0:L - 1], hw(0), hc[:, :, 1:L], ALU.mult, ALU.add)
                stt(hc[:, :, 0:L - 1], hx[:, :, 1:L], hw(2), hc[:, :, 0:L - 1], ALU.mult, ALU.add)
                for sq_ in range(nseq):
                    for c4 in range(0, TC, 4):
                        n4 = min(4, TC - c4)
                        pb = bank(2 + (c4 // 4) % 2)
                        for i in range(n4):
                            tr(pb[:, i * 128:(i + 1) * 128], hc[:, sq_, (c4 + i) * 128:(c4 + i + 1) * 128])
                        m9 = A.mark()
                        stg = A.tile([4, 128], F32)
                        cp(stg[:, 0:n4, :], pb[:, 0:n4 * 128].rr("p (a n) -> p a n", a=n4), eng="act")
                        st(s["hyt"][wh, sq_, c4:c4 + n4, :, gI * 128:(gI + 1) * 128].rearrange("c p n -> p c n"),
                           stg[:, 0:n4, :], [s["Bhyt"]])
                        A.release(m9)
                A.release(m0)

    def hyena2(s, l, groups=None):
        L, nseq = s["L"], s["nseq"]
        kind = s["name"]
        isS = kind == "S"
        TC = L // 128
        nm = s["tb"]
        hw_ = hyw[kind]
        cs_ = hw_["cs"]
        nF = 128
        NCOL = nseq * 128
        m_top = A.mark()
        w3 = A.tile([4 * cs_], F32, parts=65)
        ld(w3[0:64, :], hw_["w3"][l], part=True)
        ld(w3[64:65, :], hw_["b3"][l:l + 1, :], part=True)
        negt = A.tile([TC], F32)
        ld(negt, tabs["negt" + nm])
        rtab = None
        if not isS:
            rtab = {}
            for t4 in ("fC", "fS", "iC", "iS"):
                rtab[t4] = []
                for kc in range(TC):
                    tl = A.tile([TC, 128], BF16)
                    ld(tl, tabs[t4 + nm][kc])
                    rtab[t4].append(tl)
        for cb in (groups if groups is not None else range(1 if isS else 8)):
            g0 = cb
            m0 = A.mark()
            dec = A.tile([4, nF], F32)
            for od in range(4):
                ld(dec[:, od, :], hw_["dec"][l, od * cs_ + g0 * 128: od * cs_ + g0 * 128 + nF].partition_broadcast(128), part=True)
            m9 = A.mark()
            tmpd = A.tile([4, nF], F32)
            ts(tmpd, dec, -1.0, None, ALU.mult)
            tt(dec, dec, tmpd, ALU.max)
            A.release(m9)
            dD = A.tile([2, nF], F32)
            for o in range(2):
                ld(dD[:, o, :], hw_["d"][l, o, g0 * 128: g0 * 128 + nF].partition_broadcast(128), part=True)
            z32 = A.tile([TC, NCOL], F32)
            ub = A.tile([TC, NCOL], BF16)

            def colsrc(wh, c):
                return s["hyt"][wh, :, c, :, g0 * 128:(g0 + 1) * 128].rearrange("s p n -> p s n")

            def v3(x):
                return x.rr("p (s n) -> p s n", s=nseq)

            for c in range(TC):
                dma(v3(z32[:, c, :]).ap, colsrc(0, c), [s["Bhyt"]], z32.bufs, None, "sp", True)
            for o in range(2):
                cp(ub, z32)
                m1 = A.mark()
                hsm = A.tile([TC, nF], BF16)
                hdf = A.tile([TC, nF], BF16)
                for c in range(TC):
                    m2 = A.mark()
                    win = A.tile([2, nF], F32)
                    fw = A.tile([2, nF], F32)
                    act(win, dec[:, 2 * o:2 * o + 2, :], AF.Exp, scale=negt[:, c:c + 1])
                    for dr in range(2):
                        pb = bank(dr, nF)
                        cols = (o * 2 + dr) * cs_ + g0 * 128
                        mm(pb, hdn[nm][:, c * 128:(c + 1) * 128], w3[:, cols:cols + 128], True, True)
                        tt(fw[:, dr, :], pb, win[:, dr, :], ALU.mult)
                    if c == 0:
                        memset(fw[0:1, 1, :], 0.0)
                    tt(hsm[:, c, :], fw[:, 0, :], fw[:, 1, :], ALU.add)
                    tt(hdf[:, c, :], fw[:, 0, :], fw[:, 1, :], ALU.subtract)
                    A.release(m2)
                Yr = A.tile([TC, NCOL], BF16)
                Yi = A.tile([TC, NCOL], BF16)
                tb2 = [(A.tile([TC, 128], BF16), A.tile([TC, 128], BF16)) for _ in range(2)]
                for kc in range(TC):
                    if rtab is not None:
                        tC, tS = rtab["fC"][kc], rtab["fS"][kc]
                    else:
                        tC, tS = tb2[kc % 2]
                        ld(tC, tabs["fC" + nm][kc])
                        ld(tS, tabs["fS" + nm][kc])
                    pFr, pFi, pUr, pUi = bank(0, nF), bank(1, nF), bank(2, NCOL), bank(3, NCOL)
                    if kc % 2:
                        pFr, pFi, pUr, pUi = bank(4, nF), bank(5, nF), bank(6, NCOL), bank(7, NCOL)
                    for c in range(TC):
                        mm(pFr, tC[:, c, :], hsm[:, c, :], c == 0, c == TC - 1)
                    for c in range(TC):
                        mm(pFi, tS[:, c, :], hdf[:, c, :], c == 0, c == TC - 1)
                    for c in range(TC):
                        mm(pUr, tC[:, c, :], ub[:, c, :], c == 0, c == TC - 1)
                    for c in range(TC):
                        mm(pUi, tS[:, c, :], ub[:, c, :], c == 0, c == TC - 1)
                    m2 = A.mark()
                    Fr = A.tile([nF], F32)
                    Fi = A.tile([nF], F32)
                    cp(Fr, pFr, eng="act")
                    cp(Fi, pFi, eng="act")
                    if kc == 0:
                        pN = bank(4, nF, 1)
                        for c in range(TC):
                            mm(pN, tS[:, c, 0:1], hsm[:, c, :], c == 0, c == TC - 1)
                        cp(Fi[0:1, :], pN, eng="act")
                    Frb = Fr.rr("p (o n) -> p o n", o=1).bc([128, nseq, 128])
                    Fib = Fi.rr("p (o n) -> p o n", o=1).bc([128, nseq, 128])
                    t1 = A.tile([NCOL], F32)
                    t2 = A.tile([NCOL], F32)
                    tt(v3(t1), v3(pUr), Frb, ALU.mult)
                    tt(v3(t2), v3(pUi), Fib, ALU.mult)
                    tt(Yr[:, kc, :], t1, t2, ALU.subtract)
                    if kc == 0:
                        cp(Yr[0:1, kc, :], t1[0:1, :])
                    tt(v3(t1), v3(pUr), Fib, ALU.mult)
                    tt(v3(t2), v3(pUi), Frb, ALU.mult)
                    if kc == 0:
                        tt(v3(t1)[0:1], v3(pUi)[0:1], Fib[0:1], ALU.mult)
                        memset(t2[0:1, :], 0.0)
                    tt(Yi[:, kc, :], t1, t2, ALU.add)
                    A.release(m2)
                gts = None
                if rtab is not None:
                    gts = []
                    for tcn in range(TC):
                        gt_ = A.tile([NCOL], F32)
                        dma(v3(gt_).ap, colsrc(1 + o, tcn), [s["Bhyt"]], gt_.bufs, None, "sp")
                        gts.append(gt_)
                for tcn in range(TC):
                    if rtab is not None:
                        tC, tS = rtab["iC"][tcn], rtab["iS"][tcn]
                    else:
                        tC, tS = tb2[tcn % 2]
                        ld(tC, tabs["iC" + nm][tcn])
                        ld(tS, tabs["iS" + nm][tcn])
                    pb = bank(tcn % 2, NCOL)
                    for kc in range(TC):
                        mm(pb, tC[:, kc, :], Yr[:, kc, :], kc == 0, False)
                    for kc in range(TC):
                        mm(pb, tS[:, kc, :], Yi[:, kc, :], False, kc == TC - 1)
                    m2 = A.mark()
                    if gts is not None:
                        gt = gts[tcn]
                    else:
                        gt = A.tile([NCOL], F32)
                        dma(v3(gt).ap, colsrc(1 + o, tcn), [s["Bhyt"]], gt.bufs, None, "sp")
                    t1 = A.tile([NCOL], F32)
                    tt(v3(t1), v3(z32[:, tcn, :]), dD[:, o:o + 1, :].bc([128, nseq, 128]), ALU.mult)
                    tt(t1, t1, pb, ALU.add)
                    tt(z32[:, tcn, :], t1, gt, ALU.mult)
                    A.release(m2)
                A.release(m1)
            if isS:
                m2 = A.mark()
                zb = A.tile([TC, NCOL], BF16)
                cp(zb, z32)
                for q in range(nseq):
                    st(gin[2, q * 16:(q + 1) * 16].rearrange("c p f -> p c f"), zb[:, :, q * 128:(q + 1) * 128], [Bgin])
                A.release(m2)
            else:
                for c in range(TC):
                    for j in range(nseq):
                        pb = bank(2 + j % 2, 128)
                        tr(pb, z32[:, c, j * 128:(j + 1) * 128])
                        m2 = A.mark()
                        ot = A.tile([128], BF16)
                        cp(ot, pb, eng=("act" if j % 2 else "dve"))
                        st(s["br"][16 + g0, :, j * L + c * 128: j * L + (c + 1) * 128], ot, [s["Bbr"]])
                        A.release(m2)
            A.release(m0)
        A.release(m_top)

    BL = [(sP, 0), (sP, 1), (sO, 0)]
    NBL = len(BL)

    def xT_prefetch(dchunk):
        tiles = g.xtpool[g.xtctr % 2]
        g.xtctr += 1
        for bi, (s, b) in enumerate(BL):
            dma(tiles[bi].ap, s["xT"][dchunk * 128:(dchunk + 1) * 128, b * 512:(b + 1) * 512], [s["BxT"][b]],
                tiles[bi].bufs, None, "sp")
        return tiles

    def xT_update(dchunk, pbs, j, tiles):
        for bi, (s, b) in enumerate(BL):
            xt = tiles[bi]
            for (c0, c1, v) in s["vl"]:
                stt(xt[:, c0:c1], pbs[bi][:, c0:c1], modcol(g.l, j, dchunk, v), xt[:, c0:c1], ALU.mult, ALU.add)
            dma(s["xT"][dchunk * 128:(dchunk + 1) * 128, b * 512:(b + 1) * 512], xt.ap, xt.bufs, [s["BxT"][b]], None, "sp", True)

    def phaseB(l):
        g.l = l
        m0 = A.mark()
        hts = [A.tile([KC, 512], BF16) for _ in range(NBL)]
        for bi, (s, b) in enumerate(BL):
            ld(hts[bi], s["hT"][b], extra_reads=[s["BhT"][b]])
        brT = A.tile([24, 1536], BF16)
        dma(brT.ap[:, :, 0:1024], sP["br"].rearrange("c p n -> p c n"), [sP["Bbr"]], brT.bufs, None, "sp", True)
        m8 = A.mark()
        sel = A.tile([16, 256], BF16)
        ld(sel, sel_in)
        dbuf = [A.tile([32, 128], BF16) for _ in range(3)]
        di = 0
        for gq in range(8):
            for n in range(3):
                dd = dbuf[di % 3]
                di += 1
                ld(dd, gout[gq, n].rearrange("c p f -> p c f"), extra_reads=[Bgout])
                pb = bank(di % 2)
                for q in range(2):
                    for c in range(16):
                        mm(pb[:, q * 256:(q + 1) * 256], dd[:, q * 16 + c, :], sel[:, c, :], c == 0, c == 15)
                cp(brT[:, n * 8 + gq, 1024:1536], pb, eng=("act" if di % 2 else "dve"))
        A.release(m8)
        wgs = [A.tile([KC, 3, 128], BF16) for _ in range(2)]
        wbs = [A.tile([8, 3, 128], BF16) for _ in range(2)]
        for dc in range(KC):
            wg, wbb = wgs[dc % 2], wbs[dc % 2]
            for n in range(3):
                wload(wg[:, :, n, :], w_gate[l, :, n * D + dc * 128: n * D + (dc + 1) * 128].rearrange("(k p) m -> p k m", p=128), part=True)
                wload(wbb[:, :, n, :], w_br[l, n, :, dc * 128:(dc + 1) * 128].rearrange("(k p) m -> p k m", p=128), part=True)
            for bi in range(NBL):
                sl = slice(bi * 512, (bi + 1) * 512)
                gp = [bank(n) for n in range(3)]
                bp = [bank(3 + n) for n in range(3)]
                for n in range(3):
                    for kc in range(KC):
                        mm(gp[n], wg[:, kc, n, :], hts[bi][:, kc, :], kc == 0, kc == KC - 1)
                    for wc in range(8):
                        mm(bp[n], wbb[:, wc, n, :], brT[:, n * 8 + wc, sl], wc == 0, wc == 7)
                m1 = A.mark()
                sg = [A.tile([512], F32) for _ in range(3)]
                for n in range(3):
                    act(sg[n], gp[n], AF.Sigmoid)
                    tt(sg[n], sg[n], bp[n], ALU.mult)
                tt(sg[0], sg[0], sg[1], ALU.add)
                mo = A.tile([512], BF16)
                tt(mo, sg[0], sg[2], ALU.add)
                st(mixd[dc, :, sl], mo, [Bmixd])
                A.release(m1)
        A.release(m0)
        m0 = A.mark()
        mix = A.tile([KC, 1536], BF16)
        dma(mix.ap, mixd.rearrange("k p n -> p k n"), [Bmixd], mix.bufs, None, "sp")
        wos = [A.tile([KC, 128], BF16) for _ in range(2)]
        g.xtpool = [[A.tile([512], F32) for _ in range(NBL)] for _ in range(2)]
        g.xtctr = 0
        nxt = xT_prefetch(0)
        for dc in range(KC):
            wo = wos[dc % 2]
            wload(wo, w_out[l, :, dc * 128:(dc + 1) * 128].rearrange("(k p) m -> p k m", p=128))
            cur = nxt
            if dc + 1 < KC:
                nxt = xT_prefetch(dc + 1)
            pbs = [bank((dc % 2) * 3 + bi) for bi in range(NBL)]
            for bi in range(NBL):
                for kc in range(KC):
                    mm(pbs[bi], wo[:, kc, :], mix[:, kc, bi * 512:(bi + 1) * 512], kc == 0, kc == KC - 1)
            xT_update(dc, pbs, 2, cur)
        A.release(m0)

    def ffn(l):
        g.l = l
        m0 = A.mark()
        hts = [A.tile([KC, 512], BF16) for _ in range(NBL)]
        for bi, (s, b) in enumerate(BL):
            norm_phase(s, l, 1, [b], [hts[bi]], False)
        HC = FC // 2
        for half in range(2):
            m1 = A.mark()
            actT = A.tile([HC, 1536], BF16)
            wgu = [A.tile([KC, 2, 128], BF16) for _ in range(2)]
            for ci in range(HC):
                c = half * HC + ci
                w = wgu[ci % 2]
                wload(w[:, :, 0, :], w_ffg[l, :, c * 128:(c + 1) * 128].rearrange("(k p) m -> p k m", p=128), part=True)
                wload(w[:, :, 1, :], w_ffu[l, :, c * 128:(c + 1) * 128].rearrange("(k p) m -> p k m", p=128), part=True)
                for bi in range(NBL):
                    gp = bank(2 * bi)
                    up = bank(2 * bi + 1)
                    for kc in range(KC):
                        mm(gp, w[:, kc, 0, :], hts[bi][:, kc, :], kc == 0, kc == KC - 1)
                    for kc in range(KC):
                        mm(up, w[:, kc, 1, :], hts[bi][:, kc, :], kc == 0, kc == KC - 1)
                    m2 = A.mark()
                    sg = A.tile([512], F32)
                    act(sg, gp, AF.Silu)
                    tt(actT[:, ci, bi * 512:(bi + 1) * 512], sg, up, ALU.mult)
                    A.release(m2)
            wds = [A.tile([HC, 128], BF16) for _ in range(2)]
            g.xtpool = [[A.tile([512], F32) for _ in range(NBL)] for _ in range(2)]
            g.xtctr = 0
            nxt = xT_prefetch(0)
            for dc in range(KC):
                wd = wds[dc % 2]
                wload(wd, w_ffd[l, half * HC * 128:(half + 1) * HC * 128, dc * 128:(dc + 1) * 128].rearrange("(k p) m -> p k m", p=128))
                cur = nxt
                if dc + 1 < KC:
                    nxt = xT_prefetch(dc + 1)
                pbs = [bank((dc % 2) * 3 + bi) for bi in range(NBL)]
                for bi in range(NBL):
                    for ci in range(HC):
                        mm(pbs[bi], wd[:, ci, :], actT[:, ci, bi * 512:(bi + 1) * 512], ci == 0, ci == HC - 1)
                xT_update(dc, pbs, 5, cur)
            A.release(m1)
        A.release(m0)

    def final_norm(s):
        for b in range(s["NB"]):
            m0 = A.mark()
            yT = A.tile([KC, 512], F32)
            norm_phase(s, 0, 2, [b], [yT], False)
            for q in range(4):
                yo = A.tile([D], F32)
                for k4 in range(4):
                    pb = bank(k4 % 2)
                    for i in range(4):
                        kc = k4 * 4 + i
                        tr(pb[:, i * 128:(i + 1) * 128], yT[:, kc, q * 128:(q + 1) * 128])
                    cp(yo[:, k4 * 512:(k4 + 1) * 512], pb, eng=("act" if k4 % 2 else "dve"))
                st(yout[s["name"]][b * 512 + q * 128: b * 512 + (q + 1) * 128, :], yo, [Bout])
            A.release(m0)

    import os
    KSTOP = int(os.environ.get("KSTOP", "0"))

    class _Stop(Exception):
        pass

    def ck(n):
        if KSTOP == n + 10 * g.lcur:
            raise _Stop()

    def _main_schedule():
        load_xT_layer0(sO)
        load_xT_layer0(sP)
        for l in range(LAYERS):
            g.lcur = l
            layer_setup(l)
            m0 = A.mark()
            ho = A.tile([KC, 512], BF16)
            norm_phase(sO, l, 0, [0], [ho], True)
            A.release(m0)
            ck(1)
            allgather(sO["hT"], sS["hT"], [sO["BhT"][0]], [sS["BhT"][0]], "h%d" % l)
            ck(2)
            m0 = A.mark()
            hts = [A.tile([KC, 512], BF16) for _ in range(sP["NB"])]
            norm_phase(sP, l, 0, list(range(sP["NB"])), hts, True)
            mixers(sP, l, hts, groups=range(0, 4))
            A.release(m0)
            ck(3)
            m0 = A.mark()
            mixers(sS, l, None)
            A.release(m0)
            ck(4)
            hyena2(sS, l)
            ck(5)
            allgather(gin, gout, [Bgin], [Bgout], "b%d" % l)
            m0 = A.mark()
            hts = [A.tile([KC, 512], BF16) for _ in range(sP["NB"])]
            for b in range(sP["NB"]):
                ld(hts[b], sP["hT"][b], extra_reads=[sP["BhT"][b]])
            mixers(sP, l, hts, groups=range(4, 8))
            A.release(m0)
            hyena2(sP, l)
            m0 = A.mark()
            pb = bank(0, 128, 64)
            tr(pb, st_sb)
            so = A.tile([128], F32, parts=64)
            cp(so, pb)
            for sq_ in range(NPS):
                st(nst[sq_, l].rearrange("d (g n) -> (d g) n", n=128), so[sq_ * 16:(sq_ + 1) * 16, :], [Bout])
            A.release(m0)
            ck(6)
            phaseB(l)
            ck(7)
            ffn(l)
            ck(8)
        final_norm(sP)
        final_norm(sO)

    try:
        _main_schedule()
    except _Stop:
        pass
    P.emit(es)
    es.close()
    return nc


_CACHE = {}


def kernel(**inp):
    inp = {k: np.asarray(v) for k, v in inp.items()}
    consts = _consts()
    prm = np.stack([_prm_rows(inp, l)[0] for l in range(LAYERS)])
    idx = _prm_rows(inp, 0)[1]
    RP = prm.shape[1]
    if "nc" not in _CACHE:
        _CACHE["nc"] = build(idx, RP)
    nc = _CACHE["nc"]
    f32 = lambda a: np.ascontiguousarray(a, dtype=np.float32)
    shared = {}
    for k in ("w_in", "w_gate", "att_lambda", "lru_wa", "lru_wx", "hy_w1", "hy_w2", "hy_w3", "hy_b3",
              "hy_d", "w_br", "w_out", "w_ff_gate", "w_ff_up", "w_ff_down"):
        shared[k] = f32(inp[k])
    shared["hy_decay"] = f32(inp["hy_decay"].reshape(LAYERS, 4096))
    shared["prm"] = prm
    shared["cvecs"] = f32(np.stack([inp["c_ctx"], inp["c"][0], inp["c"][1]]))
    for k, vv in consts.items():
        shared[k] = vv
    in_maps = []
    w3r = inp["hy_w3"].reshape(LAYERS, 64, 4, 1024)
    b3r = inp["hy_b3"].reshape(LAYERS, 4, 1024)
    dcr = inp["hy_decay"].reshape(LAYERS, 4, 1024)
    for c in range(NCORES):
        m = dict(shared)
        ch = slice(c * 128, (c + 1) * 128)
        m["w_mod_s"] = f32(inp["w_mod"][:, :, c * 1536:(c + 1) * 1536])
        m["b_mod_s"] = f32(inp["b_mod"][:, c * 1536:(c + 1) * 1536])
        m["xp"] = f32(inp["x_prompt"][c * NPS:(c + 1) * NPS].reshape(NPS * LP, D))
        m["xo"] = f32(inp["x_sample"][:, c * 256:(c + 1) * 256, :].reshape(TO, D))
        m["ck"] = f32(inp["cache_k"][:, :, :, c, :])
        m["cv"] = f32(inp["cache_v"][:, :, :, c, :])
        m["s0"] = f32(inp["state_lru"][:, :, :, ch])
        cols = np.concatenate([np.arange(off + c * 128, off + (c + 1) * 128) for off in
                               (0, 1024, 2048, 3072, 4096, 5120, 6144, 7168)])
        m["w_in_s"] = f32(inp["w_in"][:, :, cols])
        m["lru_wa_s"] = f32(inp["lru_wa"][:, :, 2 * c:2 * c + 2])
        m["lru_wx_s"] = f32(inp["lru_wx"][:, :, 2 * c:2 * c + 2])
        m["hy_w3_s"] = f32(w3r[:, :, :, ch].reshape(LAYERS, 64, 512))
        m["hy_b3_s"] = f32(b3r[:, :, ch].reshape(LAYERS, 512))
        m["hy_decay_s"] = f32(dcr[:, :, ch].reshape(LAYERS, 512))
        m["hy_d_s"] = f32(inp["hy_d"][:, :, ch])
        rolled = _roll_inputs(inp, c)
        m["prm_s"] = np.stack([_prm_rows(rolled, l)[0] for l in range(LAYERS)])
        sel = np.zeros((128, 16, 256), np.float32)
        for j in range(256):
            t = 256 * c + j
            sel[t % 128, t // 128, j] = 1.0
        m["sel"] = _bf(sel)
        in_maps.append(m)
    res = run_bass_kernel_spmd(nc, in_maps, core_ids=list(range(NCORES)))
    R = res.results
    y_prompt = np.concatenate([R[c]["yp"].reshape(NPS, LP, D) for c in range(NCORES)], 0)
    y_sample = np.concatenate([R[c]["yo"].reshape(2, 256, D) for c in range(NCORES)], 1)
    new_ck = np.concatenate([R[c]["nck"] for c in range(NCORES)], 0)
    new_cv = np.concatenate([R[c]["ncv"] for c in range(NCORES)], 0)
    new_st = np.concatenate([R[c]["nst"] for c in range(NCORES)], 0)
    return (y_prompt.astype(np.float32), y_sample.astype(np.float32), new_ck.astype(np.float32),
            new_cv.astype(np.float32), new_st.astype(np.float32))
```
